# Optimizing a Trainium2 kernel written in Bass

```python
import math
import jax, jax.numpy as jnp
from jax import lax
import numpy as np

D_MODEL = 1024
BATCH = 2
SEQ = 8192
DEPTH = 4

GRID_W = 64
CTX_LEN = 256
N_MIXERS = 3
GDN_HEADS = 8
GDN_DK = 128
GDN_DV = 128
GDN_WIDTH = GDN_HEADS * GDN_DK
GDN_CHUNK = 64
SHORT_CONV = 3
SCONV_WIDTH = D_MODEL
DIFF_HEADS = 8
DIFF_DH = D_MODEL // (2 * DIFF_HEADS)
DIFF_DV = 2 * DIFF_DH
DIFF_WIDTH = DIFF_HEADS * DIFF_DV
Q_BLOCK = 128
ROPE_BASE = 10000.0
D_FF = 4 * D_MODEL
N_ADA = 6
EPS = 1e-6
N_A = (DEPTH + 2) // 3
N_B = (DEPTH + 1) // 3
N_C = DEPTH // 3

kernel_name = 'hybrid_gdn_shortconv_diffattn_dit'

F32 = jnp.float32


def rmsnorm(x, g):
    xf = x.astype(F32)
    y = xf * lax.rsqrt(jnp.mean(xf * xf, axis=-1, keepdims=True) + EPS)
    return (y * g.astype(F32)).astype(x.dtype)


def l2norm(x):
    xf = x.astype(F32)
    return (xf * lax.rsqrt(jnp.sum(xf * xf, axis=-1, keepdims=True) + EPS)).astype(x.dtype)


def modulate(h, shift, scale):
    return h * (1.0 + scale) + shift


def conv_centred(x, w):
    k = w.shape[0]
    pad = k // 2
    t = x.shape[1]
    xp = jnp.pad(x, ((0, 0), (pad, pad), (0, 0)))
    y = xp[:, 0:t] * w[0]
    for j in range(1, k):
        y = y + xp[:, j:j + t] * w[j]
    return y


def axial_rope(n_tokens, dim, dtype):
    n_rows = n_tokens // GRID_W
    row = jnp.repeat(jnp.arange(n_rows), GRID_W).astype(F32)
    col = jnp.tile(jnp.arange(GRID_W), n_rows).astype(F32)
    nf = dim // 4
    inv = ROPE_BASE ** (-jnp.arange(nf, dtype=F32) / nf)
    ang = jnp.concatenate([row[:, None] * inv, col[:, None] * inv], axis=-1)
    return jnp.cos(ang).astype(dtype), jnp.sin(ang).astype(dtype)


def apply_rope(x, cos, sin):
    xp = x.reshape(x.shape[:-1] + (x.shape[-1] // 2, 2))
    x0, x1 = xp[..., 0], xp[..., 1]
    shape = (1, cos.shape[0]) + (1,) * (x.ndim - 3) + (cos.shape[1],)
    c, s = cos.reshape(shape), sin.reshape(shape)
    return jnp.stack([x0 * c - x1 * s, x0 * s + x1 * c], axis=-1).reshape(x.shape)


def mlp_sqrelu(h, w1, w2):
    return jnp.square(jax.nn.relu(h @ w1)) @ w2


def gated_delta_chunked(q, k, v, g, beta, s0):
    out_dtype = v.dtype
    b, h, t, dk = q.shape
    dv = v.shape[-1]
    c = GDN_CHUNK
    n = t // c
    q = q.astype(F32).reshape(b, h, n, c, dk)
    k = k.astype(F32).reshape(b, h, n, c, dk)
    v = v.astype(F32).reshape(b, h, n, c, dv)
    g = g.astype(F32).reshape(b, h, n, c)
    beta = beta.astype(F32).reshape(b, h, n, c)
    gc = jnp.cumsum(g, axis=-1)
    incl = jnp.tril(jnp.ones((c, c), dtype=bool))
    strict = jnp.tril(jnp.ones((c, c), dtype=bool), -1)
    decay = jnp.where(incl, jnp.exp(jnp.where(incl, gc[..., :, None] - gc[..., None, :], 0.0)), 0.0)
    kk = jnp.einsum('bhncd,bhnsd->bhncs', k, k)
    a_mat = jnp.where(strict, beta[..., :, None] * kk * decay, 0.0) + jnp.eye(c, dtype=F32)
    rhs = jnp.concatenate([v * beta[..., None], k * (beta * jnp.exp(gc))[..., None]], axis=-1)
    sol = lax.linalg.triangular_solve(a_mat, rhs, left_side=True, lower=True, unit_diagonal=True)
    u, w = sol[..., :dv], sol[..., dv:]
    attn = jnp.where(incl, jnp.einsum('bhncd,bhnsd->bhncs', q, k) * decay, 0.0)
    qd = q * jnp.exp(gc)[..., None]
    kt = k * jnp.exp(gc[..., -1:] - gc)[..., None]
    cd = jnp.exp(gc[..., -1])
    xs = tuple(jnp.moveaxis(z, 2, 0) for z in (qd, kt, u, w, attn, cd))

    def step(s, inp):
        qd_c, kt_c, u_c, w_c, attn_c, cd_c = inp
        v_new = u_c - jnp.einsum('bhcd,bhde->bhce', w_c, s)
        o_c = jnp.einsum('bhcd,bhde->bhce', qd_c, s) + jnp.einsum('bhcs,bhse->bhce', attn_c, v_new)
        s = s * cd_c[..., None, None] + jnp.einsum('bhcd,bhce->bhde', kt_c, v_new)
        return s, o_c

    s_fin, o = lax.scan(step, s0.astype(F32), xs)
    o = jnp.moveaxis(o, 0, 2).reshape(b, h, t, dv)
    return o.astype(out_dtype), s_fin


def gdn_mixer(h_ctx, h_lat, w_in, conv_w, a_log, dt_bias, norm_g, w_out):
    def prep(hh):
        b, t, _ = hh.shape
        z = hh @ w_in
        qkv = jax.nn.silu(conv_centred(z[..., :3 * GDN_WIDTH], conv_w))
        gate = z[..., 3 * GDN_WIDTH:4 * GDN_WIDTH]
        ab = z[..., 4 * GDN_WIDTH:].reshape(b, t, 2, 2, GDN_HEADS)
        heads = lambda y: y.reshape(b, t, GDN_HEADS, -1).transpose(0, 2, 1, 3)
        q = l2norm(heads(qkv[..., :GDN_WIDTH])) * (GDN_DK ** -0.5)
        k = l2norm(heads(qkv[..., GDN_WIDTH:2 * GDN_WIDTH]))
        v = heads(qkv[..., 2 * GDN_WIDTH:])
        g = -jnp.exp(a_log.astype(F32)) * jax.nn.softplus(ab[:, :, 0].astype(F32) + dt_bias.astype(F32))
        beta = jax.nn.sigmoid(ab[:, :, 1].astype(F32))
        return q, k, v, g.transpose(2, 0, 3, 1), beta.transpose(2, 0, 3, 1), gate

    qc, kc, vc, gcx, bcx, zc = prep(h_ctx)
    ql, kl, vl, glt, blt, zl = prep(h_lat)
    s0 = jnp.zeros((h_lat.shape[0], GDN_HEADS, GDN_DK, GDN_DV), F32)
    flip = lambda y: jnp.flip(y, axis=2)
    oc_f, sc_f = gated_delta_chunked(qc, kc, vc, gcx[0], bcx[0], s0)
    ol_f, _ = gated_delta_chunked(ql, kl, vl, glt[0], blt[0], sc_f)
    oc_b, sc_b = gated_delta_chunked(flip(qc), flip(kc), flip(vc), flip(gcx[1]), flip(bcx[1]), s0)
    ol_b, _ = gated_delta_chunked(flip(ql), flip(kl), flip(vl), flip(glt[1]), flip(blt[1]), sc_b)
    o_ctx = oc_f + flip(oc_b)
    o_lat = ol_f + flip(ol_b)

    def out(o, z):
        b, h, t, dv = o.shape
        o = o.transpose(0, 2, 1, 3)
        o = rmsnorm(o, norm_g) * jax.nn.silu(z.reshape(b, t, h, dv))
        return o.reshape(b, t, h * dv) @ w_out

    return out(o_ctx, zc), out(o_lat, zl)


def short_conv_mixer(h_ctx, h_lat, w_in, conv_w, w_out):
    def mix(hh):
        z = hh @ w_in
        bg, cg, hv = jnp.split(z, 3, axis=-1)
        return (bg * conv_centred(cg * hv, conv_w)) @ w_out
    return mix(h_ctx), mix(h_lat)


def diff_attn_mixer(h_ctx, h_lat, w_in, lam, norm_g, w_out, lambda_init):
    def prep(hh):
        b, t, _ = hh.shape
        z = hh @ w_in
        q = z[..., :DIFF_WIDTH].reshape(b, t, DIFF_HEADS, 2, DIFF_DH)
        k = z[..., DIFF_WIDTH:2 * DIFF_WIDTH].reshape(b, t, DIFF_HEADS, 2, DIFF_DH)
        v = z[..., 2 * DIFF_WIDTH:].reshape(b, t, DIFF_HEADS, DIFF_DV)
        return q, k, v

    qc, kc, vc = prep(h_ctx)
    ql, kl, vl = prep(h_lat)
    b, t = h_lat.shape[0], h_lat.shape[1]
    cos, sin = axial_rope(t, DIFF_DH, ql.dtype)
    ql = apply_rope(ql, cos, sin)
    kl = apply_rope(kl, cos, sin)
    lf = lam.astype(F32)
    lmb = jnp.exp(jnp.sum(lf[0] * lf[1])) - jnp.exp(jnp.sum(lf[2] * lf[3])) + lambda_init
    scale = DIFF_DH ** -0.5

    def attend(q, k, v):
        s = jnp.einsum('bqhmd,bkhmd->bhmqk', q, k).astype(F32) * scale
        p = jax.nn.softmax(s, axis=-1)
        a = (p[:, :, 0] - lmb * p[:, :, 1]).astype(v.dtype)
        return jnp.einsum('bhqk,bkhe->bqhe', a, v)

    o_ctx = attend(qc, kc, vc)
    k_all = jnp.concatenate([kc, kl], axis=1)
    v_all = jnp.concatenate([vc, vl], axis=1)
    nb = t // Q_BLOCK
    qb = jnp.moveaxis(ql.reshape(b, nb, Q_BLOCK, DIFF_HEADS, 2, DIFF_DH), 1, 0)
    o_lat = lax.map(lambda qq: attend(qq, k_all, v_all), qb)
    o_lat = jnp.moveaxis(o_lat, 0, 1).reshape(b, t, DIFF_HEADS, DIFF_DV)

    def out(o):
        o = rmsnorm(o, norm_g) * (1.0 - lambda_init)
        return o.reshape(o.shape[0], o.shape[1], DIFF_WIDTH) @ w_out

    return out(o_ctx), out(o_lat)


def setup_inputs(seed: int = 0) -> dict:
    key = jax.random.key(seed)
    ks = jax.random.split(key, 24)
    D = D_MODEL

    def nrm(k, shape, s):
        return jax.random.normal(k, shape, F32) * s

    gdn_cols = 4 * GDN_WIDTH + 4 * GDN_HEADS
    dt = jnp.exp(jax.random.uniform(ks[13], (N_A, 2, GDN_HEADS), F32, math.log(1e-3), math.log(1e-1)))
    return {
        'x': nrm(ks[0], (BATCH, SEQ, D), 1.0),
        'c': nrm(ks[1], (BATCH, D), 1.0),
        'ctx': nrm(ks[2], (BATCH, CTX_LEN, D), 1.0),
        'c_ctx': nrm(ks[3], (D,), 1.0),
        'norm1_g': 1.0 + nrm(ks[4], (DEPTH, D), 0.02),
        'norm2_g': 1.0 + nrm(ks[5], (DEPTH, D), 0.02),
        'ada_w': nrm(ks[6], (DEPTH, D, N_ADA * D), 0.5 * D ** -0.5),
        'ada_b': nrm(ks[7], (DEPTH, N_ADA * D), 0.02),
        'mlp_w1': nrm(ks[8], (DEPTH, D, D_FF), D ** -0.5),
        'mlp_w2': nrm(ks[9], (DEPTH, D_FF, D), D_FF ** -0.5),
        'gdn_w_in': nrm(ks[10], (N_A, D, gdn_cols), D ** -0.5),
        'gdn_conv': nrm(ks[11], (N_A, SHORT_CONV, 3 * GDN_WIDTH), SHORT_CONV ** -0.5),
        'gdn_a_log': jnp.log(jax.random.uniform(ks[12], (N_A, 2, GDN_HEADS), F32, 1.0, 16.0)),
        'gdn_dt_bias': dt + jnp.log(-jnp.expm1(-dt)),
        'gdn_norm_g': 1.0 + nrm(ks[14], (N_A, GDN_DV), 0.02),
        'gdn_w_out': nrm(ks[15], (N_A, GDN_WIDTH, D), GDN_WIDTH ** -0.5),
        'sconv_w_in': nrm(ks[16], (N_B, D, 3 * SCONV_WIDTH), D ** -0.5),
        'sconv_conv': nrm(ks[17], (N_B, SHORT_CONV, SCONV_WIDTH), SHORT_CONV ** -0.5),
        'sconv_w_out': nrm(ks[18], (N_B, SCONV_WIDTH, D), SCONV_WIDTH ** -0.5),
        'diff_w_in': nrm(ks[19], (N_C, D, 3 * DIFF_WIDTH), D ** -0.5),
        'diff_lambda': nrm(ks[20], (N_C, 4, DIFF_DH), 0.1),
        'diff_norm_g': 1.0 + nrm(ks[21], (N_C, DIFF_DV), 0.02),
        'diff_w_out': nrm(ks[22], (N_C, DIFF_WIDTH, D), DIFF_WIDTH ** -0.5),
        'final_g': 1.0 + nrm(ks[23], (D,), 0.02),
    }


def reference(x, c, ctx, c_ctx, norm1_g, norm2_g, ada_w, ada_b, mlp_w1, mlp_w2,
              gdn_w_in, gdn_conv, gdn_a_log, gdn_dt_bias, gdn_norm_g, gdn_w_out,
              sconv_w_in, sconv_conv, sconv_w_out,
              diff_w_in, diff_lambda, diff_norm_g, diff_w_out, final_g):
    s_lat = jax.nn.silu(c)
    s_ctx = jax.nn.silu(c_ctx)
    x_ctx = ctx
    for i in range(DEPTH):
        kind, j = i % N_MIXERS, i // N_MIXERS
        last = i == DEPTH - 1
        mod_lat = (s_lat @ ada_w[i] + ada_b[i])[:, None, :]
        mod_ctx = (s_ctx @ ada_w[i] + ada_b[i])[None, None, :]
        sh1, sc1, g1, sh2, sc2, g2 = jnp.split(mod_lat, N_ADA, axis=-1)
        csh1, csc1, cg1, csh2, csc2, cg2 = jnp.split(mod_ctx, N_ADA, axis=-1)
        h_lat = modulate(rmsnorm(x, norm1_g[i]), sh1, sc1)
        h_ctx = modulate(rmsnorm(x_ctx, norm1_g[i]), csh1, csc1)
        if kind == 0:
            y_ctx, y_lat = gdn_mixer(h_ctx, h_lat, gdn_w_in[j], gdn_conv[j], gdn_a_log[j],
                                     gdn_dt_bias[j], gdn_norm_g[j], gdn_w_out[j])
        elif kind == 1:
            y_ctx, y_lat = short_conv_mixer(h_ctx, h_lat, sconv_w_in[j], sconv_conv[j], sconv_w_out[j])
        else:
            lambda_init = 0.8 - 0.6 * math.exp(-0.3 * i)
            y_ctx, y_lat = diff_attn_mixer(h_ctx, h_lat, diff_w_in[j], diff_lambda[j],
                                           diff_norm_g[j], diff_w_out[j], lambda_init)
        x = x + g1 * y_lat
        x = x + g2 * mlp_sqrelu(modulate(rmsnorm(x, norm2_g[i]), sh2, sc2), mlp_w1[i], mlp_w2[i])
        if not last:
            x_ctx = x_ctx + cg1 * y_ctx
            x_ctx = x_ctx + cg2 * mlp_sqrelu(modulate(rmsnorm(x_ctx, norm2_g[i]), csh2, csc2),
                                             mlp_w1[i], mlp_w2[i])
    return rmsnorm(x, final_g)
```

```python
import math
import numpy as np
from contextlib import ExitStack
import concourse.bass as bass
import concourse.mybir as mybir
from concourse.bass_utils import run_bass_kernel_spmd

F32 = mybir.dt.float32
BF16 = mybir.dt.bfloat16
AF = mybir.ActivationFunctionType
ALU = mybir.AluOpType
AX = mybir.AxisListType

D = 1024
B = 2
T = 8192
CTX = 256
DEPTH = 4
DFF = 4096
NCORE = 8
TL = T // 4
TC = CTX // 4
NT = TL + TC
EPS = 1e-6
GDN_COLS = 4128

EPOCH = 12000
ENGS = ("pe", "dve", "act", "pool", "sp")


class Buf:
    __slots__ = ("name", "last_w", "readers", "dsem", "dcount")

    def __init__(self, name):
        self.name = name
        self.last_w = None
        self.readers = []
        self.dsem = None
        self.dcount = 0


class View:
    __slots__ = ("ap", "bufs")

    def __init__(self, ap, bufs):
        self.ap = ap
        self.bufs = bufs

    def __getitem__(self, idx):
        return View(self.ap[idx], self.bufs)


class Tile:
    def __init__(self, t, buf):
        self.t = t
        self.buf = buf

    def __getitem__(self, idx):
        return View(self.t[idx], (self.buf,))

    def v(self, ap):
        return View(ap, (self.buf,))


class ColTile(Tile):
    def __init__(self, t, buf, c0, width):
        self.t = t
        self.buf = buf
        self.c0 = c0
        self.width = width

    def __getitem__(self, idx):
        if not isinstance(idx, tuple):
            idx = (idx, slice(None))
        r, c = idx
        a = 0 if c.start is None else c.start
        b = self.width if c.stop is None else c.stop
        return View(self.t[r, self.c0 + a:self.c0 + b], (self.buf,))


def carve(tile, n, width):
    return [ColTile(tile.t, Buf(f"{tile.buf.name}_{i}"), i * width, width) for i in range(n)]


def _flat(x):
    out = []
    for i in x:
        if isinstance(i, View):
            out.extend(i.bufs)
        elif isinstance(i, Tile):
            out.append(i.buf)
        elif isinstance(i, Buf):
            out.append(i)
        elif i is None:
            continue
        elif isinstance(i, (list, tuple)):
            out.extend(_flat(i))
    return out


class Sched:
    def __init__(self, nc, stack):
        self.nc = nc
        self.stack = stack
        self.ops = {e: [] for e in ENGS}
        self.count = {e: 0 for e in ENGS}
        self.known = {e: {} for e in ENGS}
        self.sems = {}
        self.nsem = 0
        self.uid = 0

    def _name(self, name):
        self.uid += 1
        return f"{name}_{self.uid}"

    def sb(self, name, shape, dtype):
        t = self.stack.enter_context(self.nc.sbuf_tensor(self._name(name), list(shape), dtype))
        return Tile(t, Buf(name))

    def ps(self, name, shape, dtype):
        t = self.stack.enter_context(self.nc.psum_tensor(self._name(name), list(shape), dtype))
        return Tile(t, Buf(name))

    def sem(self, key):
        if key not in self.sems:
            self.nsem += 1
            self.sems[key] = self.stack.enter_context(self.nc.semaphore(f"s{self.nsem}"))
        return self.sems[key]

    def _deps(self, eng, reads, writes):
        evs = []
        for b in reads:
            if b.last_w is not None:
                evs.append(b.last_w)
        for b in writes:
            if b.last_w is not None:
                evs.append(b.last_w)
            evs.extend(b.readers)
        waits = {}
        kn = self.known[eng]
        for (key, val) in evs:
            if key[0] == "eng" and key[1] == eng and eng == "pe":
                continue
            if kn.get(key, 0) >= val:
                continue
            if waits.get(key, 0) < val:
                waits[key] = val
        for key, val in waits.items():
            kn[key] = val
        return list(waits.items())

    def _record(self, ev, reads, writes):
        for b in reads:
            b.readers.append(ev)
        for b in writes:
            b.last_w = ev
            b.readers = []

    def op(self, eng, fn, reads=(), writes=()):
        reads = _flat(reads)
        writes = _flat(writes)
        waits = self._deps(eng, reads, writes)
        self.count[eng] += 1
        if self.count[eng] % EPOCH == 0:
            self.count[eng] += 1
        c = self.count[eng]
        key = ("eng", eng, c // EPOCH)
        val = c % EPOCH
        ev = (key, val)
        self.ops[eng].append((waits, fn, key, 1))
        self._record(ev, reads, writes)
        return ev

    def dma(self, queue, out, in_, sem_buf=None, **kw):
        reads, writes = [], []
        if isinstance(in_, View):
            reads += list(in_.bufs)
            in_ap = in_.ap
        else:
            in_ap = in_
        if isinstance(out, View):
            writes += list(out.bufs)
            out_ap = out.ap
        else:
            out_ap = out
        if sem_buf is None:
            sem_buf = writes[0] if writes else reads[0]
        elif isinstance(sem_buf, Tile):
            sem_buf = sem_buf.buf
        waits = self._deps(queue, reads, writes)
        if sem_buf.dsem is None:
            sem_buf.dsem = ("dma", id(sem_buf))
        sem_buf.dcount += 16
        ev = (sem_buf.dsem, sem_buf.dcount)

        def fn(e, out_ap=out_ap, in_ap=in_ap, kw=kw):
            return e.dma_start(out=out_ap, in_=in_ap, **kw)

        self.ops[queue].append((waits, fn, sem_buf.dsem, 16))
        self._record(ev, reads, writes)
        return ev

    def wait_event(self, eng, ev):
        key, val = ev
        if self.known[eng].get(key, 0) >= val:
            return
        self.known[eng][key] = val
        self.ops[eng].append(([(key, val)], None, None, 0))

    def emit(self):
        nc = self.nc
        for e in ENGS:
            for (waits, fn, key, inc) in self.ops[e]:
                for (k, v) in waits:
                    self.sem(k)
                if key is not None:
                    self.sem(key)
        engmap = {"pe": "tensor", "dve": "vector", "act": "scalar", "pool": "gpsimd", "sp": "sync"}
        with nc.Block() as block:
            for e in ENGS:
                ops = self.ops[e]

                def body(eng, ops=ops):
                    for (waits, fn, key, inc) in ops:
                        for (k, v) in waits:
                            eng.wait_ge(self.sems[k], v)
                        if fn is not None:
                            ins = fn(eng)
                            ins.then_inc(self.sems[key], inc)

                getattr(block, engmap[e])(body)


def _ap(x):
    return x.ap if isinstance(x, View) else x


def _tr(xs):
    return [x for x in xs if isinstance(x, (View, Tile))]


def mm(S, out, lhsT, rhs, start=True, stop=True, **kw):
    rd = [lhsT, rhs] + ([] if start else [out])
    return S.op("pe", lambda e: e.matmul(_ap(out), _ap(lhsT), _ap(rhs), start=start, stop=stop, **kw),
                reads=rd, writes=[out])


def transpose(S, out, in_, ident):
    return S.op("pe", lambda e: e.transpose(_ap(out), _ap(in_), _ap(ident)), reads=[in_, ident], writes=[out])


def act(S, out, in_, func, bias=None, scale=None, accum_out=None):
    kw = {}
    if bias is not None:
        kw["bias"] = _ap(bias)
    if scale is not None:
        kw["scale"] = _ap(scale)
    wr = [out]
    if accum_out is not None:
        kw["accum_out"] = _ap(accum_out)
        wr.append(accum_out)
    return S.op("act", lambda e: e.activation(_ap(out), _ap(in_), func, **kw), reads=_tr([in_, bias, scale]), writes=wr)


def tt(S, eng, out, in0, in1, op):
    return S.op(eng, lambda e: e.tensor_tensor(_ap(out), _ap(in0), _ap(in1), op), reads=[in0, in1], writes=[out])


def ts(S, eng, out, in0, s1, op0, s2=None, op1=None):
    kw = {}
    if op1 is not None:
        kw["op1"] = op1
    return S.op(eng, lambda e: e.tensor_scalar(_ap(out), _ap(in0), _ap(s1), _ap(s2) if s2 is not None else None, op0, **kw),
                reads=_tr([in0, s1, s2]), writes=[out])


def stt(S, eng, out, in0, scalar, in1, op0, op1):
    return S.op(eng, lambda e: e.scalar_tensor_tensor(_ap(out), _ap(in0), _ap(scalar), _ap(in1), op0, op1),
                reads=_tr([in0, scalar, in1]), writes=[out])


def copy(S, eng, out, in_):
    if eng == "act":
        return S.op(eng, lambda e: e.copy(_ap(out), _ap(in_)), reads=[in_], writes=[out])
    return S.op(eng, lambda e: e.tensor_copy(_ap(out), _ap(in_)), reads=[in_], writes=[out])


def memset(S, eng, out, val):
    return S.op(eng, lambda e: e.memset(_ap(out), val), reads=[], writes=[out])


def recip(S, out, in_):
    return S.op("dve", lambda e: e.reciprocal(_ap(out), _ap(in_)), reads=[in_], writes=[out])


class PsumPool:
    def __init__(self, S, n=8, shape=(128, 512), dtype=F32, name="ps"):
        self.tiles = [S.ps(f"{name}{i}", shape, dtype) for i in range(n)]
        self.i = 0

    def get(self):
        t = self.tiles[self.i % len(self.tiles)]
        self.i += 1
        return t


class Rot:
    def __init__(self, tiles):
        self.tiles = tiles
        self.i = 0

    def get(self):
        t = self.tiles[self.i % len(self.tiles)]
        self.i += 1
        return t


def new_nc():
    return bass.Bass("TRN2", target_bir_lowering=False)


def w_view(w_ap):
    return w_ap.rearrange("(kc p) n -> p kc n", p=128)


MCOLS = DEPTH * 6 * D // NCORE


def build_M():
    nc = new_nc()
    w = nc.dram_tensor("w", [D, MCOLS], F32, kind="ExternalInput").ap()
    bvec = nc.dram_tensor("b", [128, MCOLS // 128], F32, kind="ExternalInput").ap()
    cT = nc.dram_tensor("cT", [128, 8 * 3], F32, kind="ExternalInput").ap()
    out = nc.dram_tensor("mod", [128, (MCOLS // 128) * 3], F32, kind="ExternalOutput").ap()
    nch = MCOLS // 128
    with ExitStack() as st:
        S = Sched(nc, st)
        ct = S.sb("ct", [128, 24], F32)
        sg = S.sb("sg", [128, 24], F32)
        sT = S.sb("sT", [128, 8, 3], BF16)
        bt = S.sb("bt", [128, nch], F32)
        res = S.sb("res", [128, nch, 3], F32)
        S.dma("sp", ct[:], cT)
        S.dma("sp", bt[:], bvec)
        wv = w_view(w)
        wts = []
        for j in range(3):
            wt = S.sb(f"w{j}", [128, 8, 1024], BF16)
            S.dma("pool", wt[:], wv[:, :, j * 1024:(j + 1) * 1024], max_dma_last_dim=4096)
            wts.append(wt)
        act(S, sg[:], ct[:], AF.Sigmoid)
        tt(S, "dve", sT.v(sT.t[:].rearrange("p a b -> p (a b)")), ct[:], sg[:], ALU.mult)
        pp = PsumPool(S, 4, (128, 512), F32)
        for j in range(nch):
            ps = pp.get()
            wt = wts[j // 8]
            c0 = (j % 8) * 128
            for kc in range(8):
                mm(S, ps[:, 0:3], wt[:, kc, c0:c0 + 128], sT[:, kc, :], start=(kc == 0), stop=(kc == 7))
            act(S, res[:, j, :], ps[:, 0:3], AF.Identity, bias=bt[:, j:j + 1])
        ev = S.dma("sp", out, res.v(res.t[:].rearrange("p a b -> p (a b)")))
        S.wait_event("sp", ev)
        S.emit()
    return nc


def run_M(inp):
    nc = build_M()
    ada_w = inp["ada_w"].reshape(DEPTH, D, 6 * D)
    ada_b = inp["ada_b"].reshape(DEPTH, 6 * D)
    c3 = np.concatenate([inp["c"], inp["c_ctx"][None, :]], axis=0)
    cT = np.ascontiguousarray(c3.T.reshape(8, 128, 3).transpose(1, 0, 2).reshape(128, 24))
    maps = []
    for c in range(NCORE):
        L, hf = c // 2, c % 2
        w = np.ascontiguousarray(ada_w[L][:, hf * MCOLS:(hf + 1) * MCOLS])
        bb = np.ascontiguousarray(ada_b[L][hf * MCOLS:(hf + 1) * MCOLS].reshape(MCOLS // 128, 128).T)
        maps.append({"w": w, "b": bb, "cT": cT})
    res = run_bass_kernel_spmd(nc, maps, core_ids=list(range(NCORE)))
    mod = np.zeros((DEPTH, 6 * D, 3), np.float32)
    for c in range(NCORE):
        L, hf = c // 2, c % 2
        r = res.results[c]["mod"].reshape(128, MCOLS // 128, 3)
        mod[L, hf * MCOLS:(hf + 1) * MCOLS, :] = r.transpose(1, 0, 2).reshape(MCOLS, 3)
    return mod


TOK_TILES = [(0, 512), (512, 512), (1024, 512), (1536, 512), (2048, 64)]

V_G1, V_SH2, V_SC2, V_G2, V_CG1, V_CSH2, V_CSC2, V_CG2, V_N2G, V_N1G, V_SH1, V_SC1, V_CSH1, V_CSC1 = range(14)
NV = 14


def norm_mod(S, pp, ones_bf, xt, ht, w, gs, sh, tmp_sq, tmp_r, tmp_x, eps_t):
    for kc in range(8):
        act(S, tmp_sq[:, kc, :w], xt[:, kc, :w], AF.Square)
    ps = pp.get()
    for kc in range(8):
        mm(S, ps[:, :w], ones_bf[:], tmp_sq[:, kc, :w], start=(kc == 0), stop=(kc == 7))
    act(S, tmp_r[:, :w], ps[:, :w], AF.Sqrt, bias=eps_t[:, 0:1], scale=1.0 / D)
    recip(S, tmp_r[:, :w], tmp_r[:, :w])
    for kc in range(8):
        tx = tmp_x.get()
        tt(S, "dve" if kc % 2 == 0 else "pool", tx[:, :w], xt[:, kc, :w], tmp_r[:, :w], ALU.mult)
        if sh is not None:
            act(S, ht[:, kc, :w], tx[:, :w], AF.Identity, bias=sh[:, kc:kc + 1], scale=gs[:, kc:kc + 1])
        else:
            act(S, ht[:, kc, :w], tx[:, :w], AF.Copy, scale=gs[:, kc:kc + 1])


def build_T(has_prev, n_in, final):
    nc = new_nc()
    xT = nc.dram_tensor("xT", [D, NT], F32, kind="ExternalInput").ap()
    vecs = nc.dram_tensor("vecs", [128, NV * 8], F32, kind="ExternalInput").ap()
    if has_prev:
        oT = nc.dram_tensor("oT", [D, NT], F32, kind="ExternalInput").ap()
        w_out = nc.dram_tensor("w_out", [D, D], F32, kind="ExternalInput").ap()
        w1 = nc.dram_tensor("w1", [D, DFF], F32, kind="ExternalInput").ap()
        w2 = nc.dram_tensor("w2", [DFF, D], F32, kind="ExternalInput").ap()
    if n_in:
        w_in = nc.dram_tensor("w_in", [D, n_in], F32, kind="ExternalInput").ap()
        zT = nc.dram_tensor("zT", [n_in, NT], F32, kind="ExternalOutput").ap()
    if has_prev and not final:
        xo = nc.dram_tensor("xo", [D, NT], F32, kind="ExternalOutput").ap()
    if final:
        yo = nc.dram_tensor("yo", [D, TL], F32, kind="ExternalOutput").ap()
    out_evs = []
    with ExitStack() as st:
        S = Sched(nc, st)
        pp = PsumPool(S, 8)
        vt = S.sb("vecs", [128, NV, 8], F32)
        S.dma("sp", vt.v(vt.t[:].rearrange("p a b -> p (a b)")), vecs)
        ones_bf = S.sb("ones", [128, 128], BF16)
        memset(S, "dve", ones_bf[:], 1.0)
        eps_t = S.sb("eps", [128, 1], F32)
        memset(S, "dve", eps_t[:], EPS)
        xts = [S.sb(f"x{i}", [128, 8, w], F32) for i, (t0, w) in enumerate(TOK_TILES)]
        hts = [S.sb(f"h{i}", [128, 8, w], BF16) for i, (t0, w) in enumerate(TOK_TILES)]
        tmp_sq = S.sb("tsq", [128, 8, 512], BF16)
        tmp_r = S.sb("tr", [128, 512], F32)
        scr = Rot([S.sb(f"scr{i}", [128, 512], F32) for i in range(5)])
        tmp_x = scr
        stage = scr
        wblk = Rot([S.sb(f"wblk{i}", [128, 8, 512], BF16) for i in range(2)])
        xv = w_view(xT)
        for i, (t0, w) in enumerate(TOK_TILES):
            S.dma("sp" if i % 2 == 0 else "act", xts[i][:], xv[:, :, t0:t0 + w])
        gsl = S.sb("gsl", [128, 8], F32)
        gsc = S.sb("gsc", [128, 8], F32)
        is_ctx = lambda i: i == len(TOK_TILES) - 1

        if has_prev:
            ov = w_view(oT)
            for i, (t0, w) in enumerate(TOK_TILES):
                S.dma("pool", hts[i][:], ov[:, :, t0:t0 + w], max_dma_last_dim=2048)
            wo = S.sb("wo", [128, 8, D], BF16)
            S.dma("pool", wo[:], w_view(w_out), max_dma_last_dim=4096)
            for i, (t0, w) in enumerate(TOK_TILES):
                g1 = V_CG1 if is_ctx(i) else V_G1
                for dc in range(8):
                    ps = pp.get()
                    for kc in range(8):
                        mm(S, ps[:, :w], wo[:, kc, dc * 128:(dc + 1) * 128], hts[i][:, kc, :w], start=(kc == 0), stop=(kc == 7))
                    stt(S, "dve", xts[i][:, dc, :w], ps[:, :w], vt[:, g1, dc:dc + 1], xts[i][:, dc, :w], ALU.mult, ALU.add)
            stt(S, "dve", gsl[:], vt[:, V_SC2, :], 1.0, vt[:, V_N2G, :], ALU.add, ALU.mult)
            stt(S, "dve", gsc[:], vt[:, V_CSC2, :], 1.0, vt[:, V_N2G, :], ALU.add, ALU.mult)
            for i, (t0, w) in enumerate(TOK_TILES):
                if is_ctx(i):
                    norm_mod(S, pp, ones_bf, xts[i], hts[i], w, gsc, vt[:, V_CSH2, :], tmp_sq, tmp_r, tmp_x, eps_t)
                else:
                    norm_mod(S, pp, ones_bf, xts[i], hts[i], w, gsl, vt[:, V_SH2, :], tmp_sq, tmp_r, tmp_x, eps_t)
            w1b = wblk
            w2b = Rot([S.sb(f"w2b{i}", [128, 4, D], BF16) for i in range(2)])
            uts = [S.sb(f"u{i}", [128, 4, w], BF16) for i, (t0, w) in enumerate(TOK_TILES)]
            rl = scr
            w1v = w_view(w1)
            w2v = w_view(w2)
            for fb in range(8):
                a1 = w1b.get()
                a2 = w2b.get()
                S.dma("pool", a1[:], w1v[:, :, fb * 512:(fb + 1) * 512], max_dma_last_dim=2048)
                S.dma("pool", a2[:], w2v[:, fb * 4:(fb + 1) * 4, :], max_dma_last_dim=4096)
                for i, (t0, w) in enumerate(TOK_TILES):
                    g2 = V_CG2 if is_ctx(i) else V_G2
                    for fc in range(4):
                        ps = pp.get()
                        for kc in range(8):
                            mm(S, ps[:, :w], a1[:, kc, fc * 128:(fc + 1) * 128], hts[i][:, kc, :w], start=(kc == 0), stop=(kc == 7))
                        r = rl.get()
                        act(S, r[:, :w], ps[:, :w], AF.Relu)
                        tt(S, "pool", uts[i][:, fc, :w], r[:, :w], r[:, :w], ALU.mult)
                    for dc in range(8):
                        ps = pp.get()
                        for fc in range(4):
                            mm(S, ps[:, :w], a2[:, fc, dc * 128:(dc + 1) * 128], uts[i][:, fc, :w], start=(fc == 0), stop=(fc == 3))
                        stt(S, "dve", xts[i][:, dc, :w], ps[:, :w], vt[:, g2, dc:dc + 1], xts[i][:, dc, :w], ALU.mult, ALU.add)
            if not final:
                xov = w_view(xo)
                for i, (t0, w) in enumerate(TOK_TILES):
                    out_evs.append(S.dma("sp", xov[:, :, t0:t0 + w], xts[i][:]))

        if n_in:
            stt(S, "dve", gsl[:], vt[:, V_SC1, :], 1.0, vt[:, V_N1G, :], ALU.add, ALU.mult)
            stt(S, "dve", gsc[:], vt[:, V_CSC1, :], 1.0, vt[:, V_N1G, :], ALU.add, ALU.mult)
            for i, (t0, w) in enumerate(TOK_TILES):
                if is_ctx(i):
                    norm_mod(S, pp, ones_bf, xts[i], hts[i], w, gsc, vt[:, V_CSH1, :], tmp_sq, tmp_r, tmp_x, eps_t)
                else:
                    norm_mod(S, pp, ones_bf, xts[i], hts[i], w, gsl, vt[:, V_SH1, :], tmp_sq, tmp_r, tmp_x, eps_t)
            wib = wblk
            wiv = w_view(w_in)
            nblk = (n_in + 511) // 512
            k = 0
            for cb in range(nblk):
                c0 = cb * 512
                cw = min(512, n_in - c0)
                a = wib.get()
                S.dma("pool", a[:, :, :cw], wiv[:, :, c0:c0 + cw], max_dma_last_dim=2048)
                for cc in range((cw + 127) // 128):
                    m = min(128, cw - cc * 128)
                    for i, (t0, w) in enumerate(TOK_TILES):
                        ps = pp.get()
                        for kc in range(8):
                            mm(S, ps[:m, :w], a[:, kc, cc * 128:cc * 128 + m], hts[i][:, kc, :w], start=(kc == 0), stop=(kc == 7))
                        sg = stage.get()
                        if k % 2 == 0:
                            copy(S, "act", sg[:m, :w], ps[:m, :w])
                        else:
                            copy(S, "dve", sg[:m, :w], ps[:m, :w])
                        r0 = c0 + cc * 128
                        out_evs.append(S.dma("sp" if k % 2 == 0 else "act", zT[r0:r0 + m, t0:t0 + w], sg[:m, :w]))
                        k += 1
        if final:
            yv = w_view(yo)
            for i, (t0, w) in enumerate(TOK_TILES[:-1]):
                xt = xts[i]
                for kc in range(8):
                    act(S, tmp_sq[:, kc, :w], xt[:, kc, :w], AF.Square)
                ps = pp.get()
                for kc in range(8):
                    mm(S, ps[:, :w], ones_bf[:], tmp_sq[:, kc, :w], start=(kc == 0), stop=(kc == 7))
                act(S, tmp_r[:, :w], ps[:, :w], AF.Sqrt, bias=eps_t[:, 0:1], scale=1.0 / D)
                recip(S, tmp_r[:, :w], tmp_r[:, :w])
                for kc in range(8):
                    stt(S, "dve", xt[:, kc, :w], xt[:, kc, :w], vt[:, V_N1G, kc:kc + 1], tmp_r[:, :w], ALU.mult, ALU.mult)
                out_evs.append(S.dma("sp", yv[:, :, t0:t0 + w], xt[:]))
        for ev in out_evs:
            S.wait_event("sp", ev)
        S.emit()
    return nc


def chunkvec(v):
    return np.ascontiguousarray(np.asarray(v, np.float32).reshape(8, 128).T)


def make_vecs(inp, mod, li_prev, li_next, b, final=False):
    vt = np.zeros((128, NV, 8), np.float32)
    if li_prev is not None:
        m = mod[li_prev]
        sh1, sc1, g1, sh2, sc2, g2 = [m[j * D:(j + 1) * D] for j in range(6)]
        vt[:, V_G1] = chunkvec(g1[:, b]); vt[:, V_SH2] = chunkvec(sh2[:, b]); vt[:, V_SC2] = chunkvec(sc2[:, b]); vt[:, V_G2] = chunkvec(g2[:, b])
        vt[:, V_CG1] = chunkvec(g1[:, 2]); vt[:, V_CSH2] = chunkvec(sh2[:, 2]); vt[:, V_CSC2] = chunkvec(sc2[:, 2]); vt[:, V_CG2] = chunkvec(g2[:, 2])
        vt[:, V_N2G] = chunkvec(inp["norm2_g"][li_prev])
    if li_next is not None:
        m = mod[li_next]
        sh1, sc1 = m[0:D], m[D:2 * D]
        vt[:, V_N1G] = chunkvec(inp["norm1_g"][li_next])
        vt[:, V_SH1] = chunkvec(sh1[:, b]); vt[:, V_SC1] = chunkvec(sc1[:, b])
        vt[:, V_CSH1] = chunkvec(sh1[:, 2]); vt[:, V_CSC1] = chunkvec(sc1[:, 2])
    if final:
        vt[:, V_N1G] = chunkvec(inp["final_g"])
    return np.ascontiguousarray(vt.reshape(128, NV * 8))


NTOK = CTX + T
SEQ_TILES = [(0, 256, 0, 256)] + [(256 + i * 512, 512, 256, NTOK) for i in range(16)]


def build_H_sconv():
    nc = new_nc()
    zs = nc.dram_tensor("zs", [2, 3, 128, NTOK], F32, kind="ExternalInput").ap()
    cw = nc.dram_tensor("cw", [128, 2 * 3], F32, kind="ExternalInput").ap()
    oT = nc.dram_tensor("oT", [256, NTOK], F32, kind="ExternalOutput").ap()
    out_evs = []
    with ExitStack() as st:
        S = Sched(nc, st)
        cwt = S.sb("cw", [128, 2, 3], F32)
        S.dma("sp", cwt.v(cwt.t[:].rearrange("p a b -> p (a b)")), cw)
        NB = 3
        bgs = Rot([S.sb(f"bg{i}", [128, 512], F32) for i in range(NB)])
        cgs = Rot([S.sb(f"cg{i}", [128, 514], F32) for i in range(NB)])
        hvs = Rot([S.sb(f"hv{i}", [128, 514], F32) for i in range(NB)])
        accs = Rot([S.sb(f"acc{i}", [128, 512], F32) for i in range(NB)])
        k = 0
        for j in range(2):
            for (t0, w, s0, s1) in SEQ_TILES:
                bg, cg, hv, acc = bgs.get(), cgs.get(), hvs.get(), accs.get()
                lo = 1 if t0 == s0 else 0
                hi = 1 if t0 + w == s1 else 0
                q1, q2 = ("sp", "act") if k % 2 == 0 else ("act", "sp")
                S.dma(q1, bg[:, :w], zs[j, 0, :, t0:t0 + w])
                if lo:
                    memset(S, "pool", cg[:, 0:1], 0.0)
                    memset(S, "pool", hv[:, 0:1], 0.0)
                if hi:
                    memset(S, "pool", cg[:, w + 1:w + 2], 0.0)
                    memset(S, "pool", hv[:, w + 1:w + 2], 0.0)
                S.dma(q2, cg[:, lo:w + 2 - hi], zs[j, 1, :, t0 - 1 + lo:t0 + w + 1 - hi])
                S.dma(q1, hv[:, lo:w + 2 - hi], zs[j, 2, :, t0 - 1 + lo:t0 + w + 1 - hi])
                tt(S, "pool", cg[:, :w + 2], cg[:, :w + 2], hv[:, :w + 2], ALU.mult)
                ts(S, "dve", acc[:, :w], cg[:, 0:w], cwt[:, j, 0:1], ALU.mult)
                stt(S, "dve", acc[:, :w], cg[:, 1:w + 1], cwt[:, j, 1:2], acc[:, :w], ALU.mult, ALU.add)
                stt(S, "dve", acc[:, :w], cg[:, 2:w + 2], cwt[:, j, 2:3], acc[:, :w], ALU.mult, ALU.add)
                tt(S, "pool", acc[:, :w], acc[:, :w], bg[:, :w], ALU.mult)
                out_evs.append(S.dma(q2, oT[j * 128:(j + 1) * 128, t0:t0 + w], acc[:, :w]))
                k += 1
        for ev in out_evs:
            S.wait_event("sp", ev)
        S.emit()
    return nc


DIFF_LAYER = 2
LAMBDA_INIT = 0.8 - 0.6 * math.exp(-0.3 * DIFF_LAYER)


def build_H_diff(tl=T, stage=99):
    nc = new_nc()
    ntok = CTX + tl
    nkt = ntok // 128
    qT = nc.dram_tensor("qT", [2, 2, 128, tl], F32, kind="ExternalInput").ap()
    kT = nc.dram_tensor("kT", [2, 2, 128, tl], F32, kind="ExternalInput").ap()
    qcT = nc.dram_tensor("qcT", [2, 128, CTX], F32, kind="ExternalInput").ap()
    kcT = nc.dram_tensor("kcT", [2, 128, CTX], F32, kind="ExternalInput").ap()
    vtm = nc.dram_tensor("v", [2, 128, nkt * 128], F32, kind="ExternalInput").ap()
    cs = nc.dram_tensor("cs", [2, 128, tl], F32, kind="ExternalInput").ap()
    lam = nc.dram_tensor("lam", [128, 4 * 64], F32, kind="ExternalInput").ap()
    ng = nc.dram_tensor("ng", [128, 1], F32, kind="ExternalInput").ap()
    oT = nc.dram_tensor("oT", [256, ntok], F32, kind="ExternalOutput").ap()
    out_evs = []
    with ExitStack() as st:
        S = Sched(nc, st)
        ones_bf = S.sb("ones", [128, 128], BF16)
        memset(S, "dve", ones_bf[:], 1.0)
        eps_t = S.sb("eps", [128, 1], F32)
        memset(S, "dve", eps_t[:], EPS)
        lamt = S.sb("lam", [128, 4, 64], F32)
        S.dma("sp", lamt.v(lamt.t[:].rearrange("p a b -> p (a b)")), lam)
        ngt = S.sb("ng", [128, 1], F32)
        S.dma("sp", ngt[:], ng)
        lp = S.sb("lp", [128, 2, 64], F32)
        ls = S.sb("ls", [128, 2], F32)
        tt(S, "dve", lp[:, 0, :], lamt[:, 0, :], lamt[:, 1, :], ALU.mult)
        tt(S, "dve", lp[:, 1, :], lamt[:, 2, :], lamt[:, 3, :], ALU.mult)
        S.op("dve", lambda e: e.tensor_reduce(ls.t[:, :], lp.t[:, :, :], AX.X, ALU.add), reads=[lp], writes=[ls])
        act(S, ls[:], ls[:], AF.Exp)
        nlmb = S.sb("nlmb", [128, 1], F32)
        tt(S, "dve", nlmb[:], ls[:, 1:2], ls[:, 0:1], ALU.subtract)
        ts(S, "dve", nlmb[:], nlmb[:], -LAMBDA_INIT, ALU.add)
        gsc = S.sb("gsc", [128, 1], F32)
        ts(S, "dve", gsc[:], ngt[:], 1.0 - LAMBDA_INIT, ALU.mult)

        KT = S.sb("KT", [128, ntok], BF16)
        QT = S.sb("QT", [128, ntok], BF16)
        V = S.sb("V", [128, nkt, 128], BF16)
        ld = Rot([S.sb(f"ld{i}", [128, 512], F32) for i in range(12)])
        t1s = Rot([S.sb(f"t1{i}", [128, 512], F32) for i in range(4)])
        Pt = Rot([S.sb(f"P{i}", [128, 1024], BF16) for i in range(3)])
        psS = Rot([S.ps(f"psS{i}", [128, 1024], F32) for i in range(2)])
        psO = [S.ps(f"psO{i}", [128, 512], F32) for i in range(2)]
        psL = [S.ps(f"psL{i}", [128, 512], F32) for i in range(2)]
        fin = Rot([S.sb(f"fin{i}", [128, 512], F32) for i in range(6)])
        sqb = S.sb("sqb", [128, 512], BF16)

        for hh in range(2):
            S.dma("pool", KT[:, 0:CTX], kcT[hh], max_dma_last_dim=1024)
            S.dma("pool", QT[:, 0:CTX], qcT[hh], max_dma_last_dim=1024)
            S.dma("pool", V.v(V.t[:].rearrange("p a b -> p (a b)")), vtm[hh], max_dma_last_dim=8192)
            for ti in range(tl // 512):
                c0 = ti * 512
                cosb, sinb = ld.get(), ld.get()
                S.dma("sp", cosb[:], cs[0, :, c0:c0 + 512])
                S.dma("act", sinb[:], cs[1, :, c0:c0 + 512])
                for (src, dst) in ((qT, QT), (kT, KT)):
                    a, b_ = ld.get(), ld.get()
                    S.dma("sp", a[:], src[hh, 0, :, c0:c0 + 512])
                    S.dma("act", b_[:], src[hh, 1, :, c0:c0 + 512])
                    t1, t2 = t1s.get(), t1s.get()
                    tt(S, "dve", t1[:], a[:], cosb[:], ALU.mult)
                    tt(S, "pool", t2[:], b_[:], sinb[:], ALU.mult)
                    tt(S, "dve", dst[:, CTX + c0:CTX + c0 + 512], t1[:], t2[:], ALU.add)
            qtiles = [(0, CTX, CTX // 128)] + [(CTX + i * 512, 512, nkt) for i in range(tl // 512)]
            if stage == 1:
                qtiles = []
                dbg = fin.get()
                copy(S, "dve", dbg[:, :], QT[:, CTX:CTX + 512])
                out_evs.append(S.dma("sp", oT[hh * 128:(hh + 1) * 128, 0:512], dbg[:, :]))
            if stage == 2:
                qtiles = qtiles[:1]
            if stage == 3:
                qtiles = qtiles[:2]
            for (q0, qw, nk) in qtiles:
                def qk(kt):
                    ps = psS.get()
                    for m in range(2):
                        mm(S, ps[:, m * 512:m * 512 + qw], KT[m * 64:(m + 1) * 64, kt * 128:(kt + 1) * 128],
                           QT[m * 64:(m + 1) * 64, q0:q0 + qw])
                    p = Pt.get()
                    if qw == 512:
                        act(S, p[:, :], ps[:, :], AF.Exp, scale=0.125)
                    else:
                        for m in range(2):
                            act(S, p[:, m * 512:m * 512 + qw], ps[:, m * 512:m * 512 + qw], AF.Exp, scale=0.125)
                    return p

                def av(kt, p):
                    for m in range(2):
                        mm(S, psO[m][:, :qw], V[:, kt, :], p[:, m * 512:m * 512 + qw], start=(kt == 0), stop=(kt == nk - 1))
                        mm(S, psL[m][:, :qw], ones_bf[:], p[:, m * 512:m * 512 + qw], start=(kt == 0), stop=(kt == nk - 1))

                prev = None
                for kt in range(nk + 1):
                    cur = qk(kt) if kt < nk else None
                    if prev is not None:
                        av(kt - 1, prev)
                    prev = cur
                r1, r2, a1, a2 = fin.get(), fin.get(), fin.get(), fin.get()
                recip(S, r1[:, :qw], psL[0][:, :qw])
                recip(S, r2[:, :qw], psL[1][:, :qw])
                tt(S, "dve", a1[:, :qw], psO[0][:, :qw], r1[:, :qw], ALU.mult)
                tt(S, "dve", a2[:, :qw], psO[1][:, :qw], r2[:, :qw], ALU.mult)
                stt(S, "dve", a1[:, :qw], a2[:, :qw], nlmb[:, 0:1], a1[:, :qw], ALU.mult, ALU.add)
                act(S, sqb[:, :qw], a1[:, :qw], AF.Square)
                pss = psS.get()
                mm(S, pss[:, :qw], ones_bf[:], sqb[:, :qw])
                act(S, r1[:, :qw], pss[:, :qw], AF.Sqrt, bias=eps_t[:, 0:1], scale=1.0 / 128)
                recip(S, r1[:, :qw], r1[:, :qw])
                stt(S, "dve", a2[:, :qw], a1[:, :qw], gsc[:, 0:1], r1[:, :qw], ALU.mult, ALU.mult)
                out_evs.append(S.dma("sp", oT[hh * 128:(hh + 1) * 128, q0:q0 + qw], a2[:, :qw]))
        for ev in out_evs:
            S.wait_event("sp", ev)
        S.emit()
    return nc


def rope_tables(tl=T):
    n_rows = tl // 64
    row = np.repeat(np.arange(n_rows), 64).astype(np.float32)
    col = np.tile(np.arange(64), n_rows).astype(np.float32)
    nf = 16
    inv = (np.float32(10000.0) ** (-np.arange(nf, dtype=np.float32) / nf)).astype(np.float32)
    ang = np.concatenate([row[:, None] * inv, col[:, None] * inv], axis=-1).astype(np.float32)
    cos = np.cos(ang).astype(np.float32)
    sin = np.sin(ang).astype(np.float32)
    cs = np.zeros((2, 128, tl), np.float32)
    for m in range(2):
        for d in range(64):
            i = d // 2
            cs[0, m * 64 + d] = cos[:, i]
            cs[1, m * 64 + d] = -sin[:, i] if d % 2 == 0 else sin[:, i]
    return cs


def swap_pairs_rows(a):
    idx = np.arange(128) ^ 1
    return a[..., idx, :]


BIG = 30000.0
CM_ID, CM_ONES, CM_MF, CM_MB, CM_BS0, CM_BS1, CM_POSF, CM_NEGF, CM_POSB, CM_NEGB = range(10)
CM_MU = 10
CM_ML = 16
NCM = 22


def gdn_consts():
    i = np.arange(128)[:, None]
    j = np.arange(128)[None, :]
    same = (i // 64) == (j // 64)
    cm = np.zeros((NCM, 128, 128), np.float32)
    cm[CM_ID] = (i == j)
    cm[CM_ONES] = 1.0
    cm[CM_MF] = same & (i <= j)
    cm[CM_MB] = same & (i >= j)
    cm[CM_BS0] = np.broadcast_to(i < 64, (128, 128))
    cm[CM_BS1] = np.broadcast_to(i >= 64, (128, 128))
    cm[CM_POSF] = np.where(same & (j < i), 0.0, BIG)
    cm[CM_NEGF] = np.where(same & (i <= j), 0.0, -BIG)
    cm[CM_POSB] = np.where(same & (j > i), 0.0, BIG)
    cm[CM_NEGB] = np.where(same & (i >= j), 0.0, -BIG)
    for l in range(6):
        sz = 1 << l
        mu = ((i // (2 * sz)) == (j // (2 * sz))) & ((i % (2 * sz)) < sz) & ((j % (2 * sz)) >= sz)
        cm[CM_MU + l] = mu
        cm[CM_ML + l] = mu.T
    return cm


def build_H_gdn(tl=T):
    nc = new_nc()
    ntok = CTX + tl
    ntile = ntok // 128
    zin = nc.dram_tensor("z", [2, 4, 128, ntok], F32, kind="ExternalInput").ap()
    abin = nc.dram_tensor("ab", [2, 128, ntile * 4], F32, kind="ExternalInput").ap()
    hc = nc.dram_tensor("hc", [2, 128, 16], F32, kind="ExternalInput").ap()
    cmin = nc.dram_tensor("cm", [NCM, 128, 128], F32, kind="ExternalInput").ap()
    oT = nc.dram_tensor("oT", [256, ntok], F32, kind="ExternalOutput").ap()
    out_evs = []
    seqs = [(0, CTX), (CTX, ntok)]
    with ExitStack() as st:
        S = Sched(nc, st)
        cm = [S.sb(f"cm{i}", [128, 128], F32) for i in range(NCM)]
        for i in range(NCM):
            S.dma("sp" if i % 2 == 0 else "act", cm[i][:], cmin[i])
        id_bf = S.sb("idbf", [128, 128], BF16)
        copy(S, "dve", id_bf[:], cm[CM_ID][:])
        ones_bf = S.sb("onesbf", [128, 128], BF16)
        memset(S, "dve", ones_bf[:], 1.0)
        lvl_bf = []
        for i in range(12):
            t_ = S.sb(f"lvl{i}", [128, 128], BF16)
            copy(S, "dve" if i % 2 == 0 else "pool", t_[:], cm[CM_MU + i][:])
            lvl_bf.append(t_)
        MU_bf, ML_bf = lvl_bf[0:6], lvl_bf[6:12]
        eps_t = S.sb("eps", [128, 1], F32)
        memset(S, "dve", eps_t[:], EPS)
        one_t = S.sb("one", [128, 1], F32)
        memset(S, "dve", one_t[:], 1.0)

        QT = S.sb("QT", [128, ntok], BF16)
        KT = S.sb("KT", [128, ntok], BF16)
        VT = S.sb("VT", [128, ntok], BF16)
        OACC = S.sb("OACC", [128, ntok], F32)
        hct = S.sb("hc", [128, 16], F32)
        abt = S.sb("ab", [128, ntile, 4], F32)
        G = S.sb("G", [128, 2, ntile], F32)
        BETA = S.sb("BETA", [128, 2, ntile], F32)
        CUM = S.sb("CUM", [128, 2, ntile], F32)
        KBS = S.sb("KBS", [128, 2, ntile], F32)
        KTS = S.sb("KTS", [128, 2, ntile], F32)
        CD = S.sb("CD", [128, 2, 2, ntile], F32)
        nea = S.sb("nea", [128, 2], F32)

        ld = Rot([S.sb(f"ld{i}", [128, 514], F32) for i in range(4)])
        cacc = Rot([S.sb(f"cacc{i}", [128, 512], F32) for i in range(4)])
        sqb = Rot([S.sb(f"sqb{i}", [128, 512], BF16) for i in range(2)])
        banks = [S.ps(f"bank{i}", [128, 512], F32) for i in range(7)]
        ps512 = Rot(banks)
        psg = Rot([ColTile(b.t, b.buf, 0, 128) for b in banks])
        tb = S.ps("pstb", [128, 1024], BF16)
        pst = Rot([ColTile(tb.t, tb.buf, i * 128, 128) for i in range(4)])
        f128 = Rot([S.sb(f"f128_{i}", [128, 128], F32) for i in range(8)])
        b128 = Rot([S.sb(f"b128_{i}", [128, 128], BF16) for i in range(52)])
        Sf = S.sb("Sf", [128, 128], F32)
        Sb = Rot([S.sb(f"Sb{i}", [128, 128], BF16) for i in range(2)])
        vn = S.sb("vn", [128, 128], BF16)
        memset(S, "dve", vn[:], 0.0)

        for hh in range(2):
            S.dma("sp", hct[:], hc[hh])
            S.dma("act", abt.v(abt.t[:].rearrange("p a b -> p (a b)")), abin[hh])
            for xi, dst in ((0, QT), (1, KT), (2, VT)):
                for (s0, s1) in seqs:
                    for t0 in range(s0, s1, 512):
                        w = min(512, s1 - t0)
                        buf = ld.get()
                        lo = 1 if t0 == s0 else 0
                        hi = 1 if t0 + w == s1 else 0
                        if lo:
                            memset(S, "pool", buf[:, 0:1], 0.0)
                        if hi:
                            memset(S, "pool", buf[:, w + 1:w + 2], 0.0)
                        S.dma("sp", buf[:, lo:w + 2 - hi], zin[hh, xi, :, t0 - 1 + lo:t0 + w + 1 - hi])
                        acc = cacc.get()
                        ts(S, "pool", acc[:, :w], buf[:, 0:w], hct[:, 3 * xi:3 * xi + 1], ALU.mult)
                        stt(S, "dve", acc[:, :w], buf[:, 1:w + 1], hct[:, 3 * xi + 1:3 * xi + 2], acc[:, :w], ALU.mult, ALU.add)
                        stt(S, "dve", acc[:, :w], buf[:, 2:w + 2], hct[:, 3 * xi + 2:3 * xi + 3], acc[:, :w], ALU.mult, ALU.add)
                        if xi == 2:
                            act(S, dst[:, t0:t0 + w], acc[:, :w], AF.Silu)
                        else:
                            act(S, acc[:, :w], acc[:, :w], AF.Silu)
                            sq = sqb.get()
                            tt(S, "pool", sq[:, :w], acc[:, :w], acc[:, :w], ALU.mult)
                            ps = ps512.get()
                            mm(S, ps[:, :w], ones_bf[:], sq[:, :w])
                            r = cacc.get()
                            act(S, r[:, :w], ps[:, :w], AF.Sqrt, bias=eps_t[:, 0:1], scale=1.0)
                            recip(S, r[:, :w], r[:, :w])
                            if xi == 0:
                                stt(S, "dve", dst[:, t0:t0 + w], acc[:, :w], 128.0 ** -0.5, r[:, :w], ALU.mult, ALU.mult)
                            else:
                                tt(S, "dve", dst[:, t0:t0 + w], acc[:, :w], r[:, :w], ALU.mult)
            act(S, nea[:], hct[:, 9:11], AF.Exp)
            ts(S, "dve", nea[:], nea[:], -1.0, ALU.mult)
            for d in range(2):
                act(S, G[:, d, :], abt[:, :, d], AF.Exp, bias=hct[:, 11 + d:12 + d])
                act(S, G[:, d, :], G[:, d, :], AF.Ln, bias=one_t[:, 0:1])
                ts(S, "dve", G[:, d, :], G[:, d, :], nea[:, d:d + 1], ALU.mult)
                act(S, BETA[:, d, :], abt[:, :, 2 + d], AF.Sigmoid)
            for d in range(2):
                ps = ps512.get()
                mm(S, ps[:, 0:ntile], cm[CM_MF if d == 0 else CM_MB][:], G[:, d, :])
                mm(S, ps[:, 128:128 + ntile], cm[CM_BS0][:], G[:, d, :])
                mm(S, ps[:, 256:256 + ntile], cm[CM_BS1][:], G[:, d, :])
                copy(S, "dve", CUM[:, d, :], ps[:, 0:ntile])
                act(S, CD[:, d, 0, :], ps[:, 128:128 + ntile], AF.Exp)
                act(S, CD[:, d, 1, :], ps[:, 256:256 + ntile], AF.Exp)
                tmp = f128.get()
                tt(S, "dve", tmp[0:64, 0:ntile], ps[0:64, 128:128 + ntile], CUM[0:64, d, :], ALU.subtract)
                tt(S, "dve", tmp[64:128, 0:ntile], ps[64:128, 256:256 + ntile], CUM[64:128, d, :], ALU.subtract)
                act(S, KTS[:, d, :], tmp[:, 0:ntile], AF.Exp)
                act(S, KBS[:, d, :], CUM[:, d, :], AF.Exp)
                tt(S, "dve", KBS[:, d, :], KBS[:, d, :], BETA[:, d, :], ALU.mult)

            for d in range(2):
                POS = cm[CM_POSF if d == 0 else CM_POSB]
                NEG = cm[CM_NEGF if d == 0 else CM_NEGB]
                memset(S, "dve", Sf[:], 0.0)
                sb_cur = Sb.get()
                memset(S, "pool", sb_cur[:], 0.0)
                nct = CTX // 128
                if d == 0:
                    order = list(range(ntile))
                else:
                    order = list(range(nct - 1, -1, -1)) + list(range(ntile - 1, nct - 1, -1))
                for ti in order:
                    c0 = ti * 128
                    cum = CUM[:, d, ti:ti + 1]
                    dg = f128.get()
                    ts(S, "pool", dg[:], cm[CM_ID][:], cum, ALU.mult)
                    psR = psg.get()
                    mm(S, psR[:], cm[CM_ONES][:], dg[:])
                    a1 = f128.get()
                    stt(S, "dve", a1[:], psR[:], cum, POS[:], ALU.subtract, ALU.add)
                    act(S, a1[:], a1[:], AF.Exp, scale=-1.0)
                    a2 = f128.get()
                    stt(S, "dve", a2[:], psR[:], cum, NEG[:], ALU.subtract, ALU.add)
                    act(S, a2[:], a2[:], AF.Exp)
                    eb = f128.get()
                    act(S, eb[:], psR[:], AF.Exp)
                    qd = b128.get()
                    tt(S, "pool", qd[:], QT[:, c0:c0 + 128], eb[:], ALU.mult)
                    pkk = psg.get()
                    mm(S, pkk[:], KT[:, c0:c0 + 128], KT[:, c0:c0 + 128])
                    pqk = psg.get()
                    mm(S, pqk[:], KT[:, c0:c0 + 128], QT[:, c0:c0 + 128])
                    A = b128.get()
                    stt(S, "dve", A[:], pkk[:], BETA[:, d, ti:ti + 1], a1[:], ALU.mult, ALU.mult)
                    attnT = b128.get()
                    tt(S, "dve", attnT[:], pqk[:], a2[:], ALU.mult)
                    ptB = pst.get()
                    transpose(S, ptB[:], A[:], id_bf[:])
                    Bm = b128.get()
                    copy(S, "act", Bm[:], ptB[:])
                    ptK = pst.get()
                    transpose(S, ptK[:], KT[:, c0:c0 + 128], id_bf[:])
                    kbg = b128.get()
                    act(S, kbg[:], ptK[:], AF.Copy, scale=KBS[:, d, ti:ti + 1])
                    ktm = b128.get()
                    act(S, ktm[:], ptK[:], AF.Copy, scale=KTS[:, d, ti:ti + 1])
                    ptV = pst.get()
                    transpose(S, ptV[:], VT[:, c0:c0 + 128], id_bf[:])
                    bv = b128.get()
                    ts(S, "dve", bv[:], ptV[:], BETA[:, d, ti:ti + 1], ALU.mult)
                    mB = MU_bf if d == 0 else ML_bf
                    mA = ML_bf if d == 0 else MU_bf
                    Al, Bl = [], []
                    for l in range(6):
                        t_a, t_b = b128.get(), b128.get()
                        tt(S, "pool", t_a[:], A[:], mA[l][:], ALU.mult)
                        tt(S, "pool", t_b[:], Bm[:], mB[l][:], ALU.mult)
                        Al.append(t_a)
                        Bl.append(t_b)
                    X = b128.get()
                    XT = b128.get()
                    tt(S, "dve", X[:], id_bf[:], Bl[0][:], ALU.subtract)
                    tt(S, "pool", XT[:], id_bf[:], Al[0][:], ALU.subtract)
                    for l in range(1, 6):
                        last = l == 5
                        pC = psg.get()
                        mm(S, pC[:], Al[l][:], X[:])
                        E = b128.get()
                        tt(S, "dve", E[:], id_bf[:], pC[:], ALU.subtract)
                        pX = psg.get()
                        mm(S, pX[:], XT[:], E[:])
                        Xn = b128.get()
                        copy(S, "act", Xn[:], pX[:])
                        if not last:
                            pC2 = psg.get()
                            mm(S, pC2[:], Bl[l][:], XT[:])
                            E2 = b128.get()
                            tt(S, "dve", E2[:], id_bf[:], pC2[:], ALU.subtract)
                            pXT = psg.get()
                            mm(S, pXT[:], X[:], E2[:])
                            XTn = b128.get()
                            copy(S, "act", XTn[:], pXT[:])
                            XT = XTn
                        X = Xn
                    pW = psg.get()
                    mm(S, pW[:], kbg[:], X[:])
                    nwT = b128.get()
                    act(S, nwT[:], pW[:], AF.Copy, scale=-1.0)
                    pO = psg.get()
                    for j in ((0, 1) if d == 0 else (1, 0)):
                        r0 = 64 * j
                        pV = psg.get()
                        mm(S, pV[:], X[:], bv[:], start=True, stop=False)
                        mm(S, pV[:], nwT[:], sb_cur[:], start=False, stop=True)
                        copy(S, "act", vn[r0:r0 + 64, :], pV[r0:r0 + 64, :])
                        mm(S, pO[:, r0:r0 + 64], sb_cur[:], qd[:, r0:r0 + 64], start=True, stop=False)
                        mm(S, pO[:, r0:r0 + 64], vn[:], attnT[:, r0:r0 + 64], start=False, stop=True)
                        pS = psg.get()
                        mm(S, pS[:], ktm[r0:r0 + 64, :], vn[r0:r0 + 64, :])
                        stt(S, "dve", Sf[:], Sf[:], CD[:, d, j, ti:ti + 1], pS[:], ALU.mult, ALU.add)
                        sb_cur = Sb.get()
                        copy(S, "pool", sb_cur[:], Sf[:])
                    if d == 0:
                        copy(S, "act", OACC[:, c0:c0 + 128], pO[:])
                    else:
                        tt(S, "dve", OACC[:, c0:c0 + 128], pO[:], OACC[:, c0:c0 + 128], ALU.add)
            for t0 in range(0, ntok, 512):
                w = min(512, ntok - t0)
                zg = ld.get()
                S.dma("act", zg[:, :w], zin[hh, 3, :, t0:t0 + w])
                sq = sqb.get()
                tt(S, "pool", sq[:, :w], OACC[:, t0:t0 + w], OACC[:, t0:t0 + w], ALU.mult)
                ps = ps512.get()
                mm(S, ps[:, :w], ones_bf[:], sq[:, :w])
                r = cacc.get()
                act(S, r[:, :w], ps[:, :w], AF.Sqrt, bias=eps_t[:, 0:1], scale=1.0 / 128)
                recip(S, r[:, :w], r[:, :w])
                act(S, zg[:, :w], zg[:, :w], AF.Silu)
                o = cacc.get()
                stt(S, "dve", o[:, :w], OACC[:, t0:t0 + w], hct[:, 15:16], r[:, :w], ALU.mult, ALU.mult)
                tt(S, "pool", o[:, :w], o[:, :w], zg[:, :w], ALU.mult)
                out_evs.append(S.dma("sp", oT[hh * 128:(hh + 1) * 128, t0:t0 + w], o[:, :w]))
        for ev in out_evs:
            S.wait_event("sp", ev)
        S.emit()
    return nc


_PROGS = {}
_DBG = None


def _prog(key, fn):
    if key not in _PROGS:
        _PROGS[key] = fn()
    return _PROGS[key]


def _run(nc, maps):
    res = run_bass_kernel_spmd(nc, maps, core_ids=list(range(NCORE)))
    return res.results


def _c(a):
    return np.ascontiguousarray(a, dtype=np.float32)


def _batch_assemble(per_core, nrows):
    out = []
    for b in range(B):
        Z = np.empty((nrows, NTOK), np.float32)
        for r in range(4):
            zc = per_core[b * 4 + r]
            Z[:, r * TC:(r + 1) * TC] = zc[:, TL:]
            Z[:, CTX + r * TL:CTX + (r + 1) * TL] = zc[:, :TL]
        out.append(Z)
    return out


def _token_split(Ob):
    out = []
    for c in range(NCORE):
        b, r = c // 4, c % 4
        out.append(_c(np.concatenate([Ob[b][:, CTX + r * TL:CTX + (r + 1) * TL], Ob[b][:, r * TC:(r + 1) * TC]], axis=1)))
    return out


def _run_H_gdn(inp, j, Zb):
    nc = _prog("Hg", build_H_gdn)
    cmc = gdn_consts()
    ntile = NTOK // 128
    maps = []
    for c in range(NCORE):
        b, r = c // 4, c % 4
        Z = Zb[b]
        z = np.empty((2, 4, 128, NTOK), np.float32)
        ab = np.empty((2, 128, ntile * 4), np.float32)
        hc = np.zeros((2, 128, 16), np.float32)
        for hh in range(2):
            h = 2 * r + hh
            for xi in range(4):
                z[hh, xi] = Z[xi * 1024 + h * 128:xi * 1024 + (h + 1) * 128]
            rows = [4096 + h, 4096 + 8 + h, 4096 + 16 + h, 4096 + 24 + h]
            abt = Z[rows].T
            ab[hh] = abt.reshape(ntile, 128, 4).transpose(1, 0, 2).reshape(128, ntile * 4)
            for xi in range(3):
                for tap in range(3):
                    hc[hh, :, 3 * xi + tap] = inp["gdn_conv"][j, tap, xi * 1024 + h * 128:xi * 1024 + (h + 1) * 128]
            hc[hh, :, 9:11] = inp["gdn_a_log"][j, :, h][None, :]
            hc[hh, :, 11:13] = inp["gdn_dt_bias"][j, :, h][None, :]
            hc[hh, :, 15] = inp["gdn_norm_g"][j]
        maps.append({"z": z, "ab": ab, "hc": hc, "cm": cmc})
    return _run(nc, maps)


def _run_H_sconv(inp, Zb):
    nc = _prog("Hs", build_H_sconv)
    maps = []
    for c in range(NCORE):
        b, r = c // 4, c % 4
        Z = Zb[b]
        zs = np.empty((2, 3, 128, NTOK), np.float32)
        cw = np.empty((128, 2, 3), np.float32)
        for jj in range(2):
            ch0 = 256 * r + 128 * jj
            for part in range(3):
                zs[jj, part] = Z[part * 1024 + ch0:part * 1024 + ch0 + 128]
            cw[:, jj, :] = inp["sconv_conv"][0][:, ch0:ch0 + 128].T
        maps.append({"zs": zs, "cw": _c(cw.reshape(128, 6))})
    return _run(nc, maps)


def _run_H_diff(inp, Zb):
    nc = _prog("Hd", build_H_diff)
    cs = rope_tables(T)
    nkt = NTOK // 128
    lam = _c(np.broadcast_to(inp["diff_lambda"][0].reshape(1, 256), (128, 256)))
    ng = _c(inp["diff_norm_g"][0].reshape(128, 1))
    maps = []
    for c in range(NCORE):
        b, r = c // 4, c % 4
        Z = Zb[b]
        qT = np.empty((2, 2, 128, T), np.float32)
        kT = np.empty((2, 2, 128, T), np.float32)
        qcT = np.empty((2, 128, CTX), np.float32)
        kcT = np.empty((2, 128, CTX), np.float32)
        v = np.empty((2, 128, nkt * 128), np.float32)
        for hh in range(2):
            h = 2 * r + hh
            q = Z[h * 128:(h + 1) * 128]
            k = Z[1024 + h * 128:1024 + (h + 1) * 128]
            vv = Z[2048 + h * 128:2048 + (h + 1) * 128]
            qT[hh, 0] = q[:, CTX:]
            qT[hh, 1] = swap_pairs_rows(q[:, CTX:])
            kT[hh, 0] = k[:, CTX:]
            kT[hh, 1] = swap_pairs_rows(k[:, CTX:])
            qcT[hh] = q[:, :CTX]
            kcT[hh] = k[:, :CTX]
            v[hh] = vv.T.reshape(nkt, 128, 128).transpose(1, 0, 2).reshape(128, nkt * 128)
        maps.append({"qT": qT, "kT": kT, "qcT": qcT, "kcT": kcT, "v": v, "cs": cs, "lam": lam, "ng": ng})
    return _run(nc, maps)


def kernel(x, c, ctx, c_ctx, norm1_g, norm2_g, ada_w, ada_b, mlp_w1, mlp_w2,
           gdn_w_in, gdn_conv, gdn_a_log, gdn_dt_bias, gdn_norm_g, gdn_w_out,
           sconv_w_in, sconv_conv, sconv_w_out,
           diff_w_in, diff_lambda, diff_norm_g, diff_w_out, final_g):
    inp = dict(x=x, c=c, ctx=ctx, c_ctx=c_ctx, norm1_g=norm1_g, norm2_g=norm2_g, ada_w=ada_w, ada_b=ada_b,
               mlp_w1=mlp_w1, mlp_w2=mlp_w2, gdn_w_in=gdn_w_in, gdn_conv=gdn_conv, gdn_a_log=gdn_a_log,
               gdn_dt_bias=gdn_dt_bias, gdn_norm_g=gdn_norm_g, gdn_w_out=gdn_w_out, sconv_w_in=sconv_w_in,
               sconv_conv=sconv_conv, sconv_w_out=sconv_w_out, diff_w_in=diff_w_in, diff_lambda=diff_lambda,
               diff_norm_g=diff_norm_g, diff_w_out=diff_w_out, final_g=final_g)
    inp = {k: np.asarray(v, dtype=np.float32) for k, v in inp.items()}
    mod = run_M(inp)
    kinds = [i % 3 for i in range(DEPTH)]
    w_ins = [inp["gdn_w_in"][0], inp["sconv_w_in"][0], inp["diff_w_in"][0], inp["gdn_w_in"][1]]
    w_outs = [inp["gdn_w_out"][0], inp["sconv_w_out"][0], inp["diff_w_out"][0], inp["gdn_w_out"][1]]
    xs = []
    for cc in range(NCORE):
        b, r = cc // 4, cc % 4
        xs.append(_c(np.concatenate([inp["x"][b, r * TL:(r + 1) * TL].T, inp["ctx"][b, r * TC:(r + 1) * TC].T], axis=1)))
    out = None
    for i in range(DEPTH + 1):
        has_prev = i > 0
        final = i == DEPTH
        n_in = 0 if final else w_ins[i].shape[1]
        nc = _prog(("T", has_prev, n_in, final), lambda: build_T(has_prev, n_in, final))
        maps = []
        for cc in range(NCORE):
            b = cc // 4
            m = {"xT": xs[cc], "vecs": make_vecs(inp, mod, i - 1 if has_prev else None, None if final else i, b, final)}
            if has_prev:
                m["oT"] = os_[cc]
                m["w_out"] = _c(w_outs[i - 1])
                m["w1"] = _c(inp["mlp_w1"][i - 1])
                m["w2"] = _c(inp["mlp_w2"][i - 1])
            if n_in:
                m["w_in"] = _c(w_ins[i])
            maps.append(m)
        res = _run(nc, maps)
        if final:
            out = np.empty((B, T, D), np.float32)
            for cc in range(NCORE):
                b, r = cc // 4, cc % 4
                out[b, r * TL:(r + 1) * TL] = res[cc]["yo"].T
            break
        if has_prev:
            xs = [_c(res[cc]["xo"]) for cc in range(NCORE)]
        Zb = _batch_assemble([res[cc]["zT"] for cc in range(NCORE)], n_in)
        if _DBG is not None:
            _DBG[f"Z{i}"] = np.stack(Zb)
            _DBG[f"X{i}"] = np.stack(_batch_assemble(xs, D))
        if kinds[i] == 0:
            hres = _run_H_gdn(inp, i // 3, Zb)
        elif kinds[i] == 1:
            hres = _run_H_sconv(inp, Zb)
        else:
            hres = _run_H_diff(inp, Zb)
        Ob = []
        for b in range(B):
            Ob.append(np.concatenate([hres[b * 4 + r]["oT"] for r in range(4)], axis=0))
        os_ = _token_split(Ob)
        if _DBG is not None:
            _DBG[f"O{i}"] = np.stack(Ob)
    return out
```

```python
import math
import numpy as np
from contextlib import ExitStack
import concourse.bass as bass
import concourse.mybir as mybir
from concourse.bass_utils import run_bass_kernel_spmd

F32 = mybir.dt.float32
BF16 = mybir.dt.bfloat16
AF = mybir.ActivationFunctionType
ALU = mybir.AluOpType
AX = mybir.AxisListType

D = 1024
B = 2
T = 8192
CTX = 256
DEPTH = 4
DFF = 4096
NCORE = 8
TL = T // 4
TC = CTX // 4
NT = TL + TC
EPS = 1e-6
GDN_COLS = 4128

EPOCH = 12000
ENGS = ("pe", "dve", "act", "pool", "sp")


class Buf:
    __slots__ = ("name", "last_w", "readers", "dsem", "dcount")

    def __init__(self, name):
        self.name = name
        self.last_w = None
        self.readers = []
        self.dsem = None
        self.dcount = 0


class View:
    __slots__ = ("ap", "bufs")

    def __init__(self, ap, bufs):
        self.ap = ap
        self.bufs = bufs

    def __getitem__(self, idx):
        return View(self.ap[idx], self.bufs)


class Tile:
    def __init__(self, t, buf):
        self.t = t
        self.buf = buf

    def __getitem__(self, idx):
        return View(self.t[idx], (self.buf,))

    def v(self, ap):
        return View(ap, (self.buf,))


class ColTile(Tile):
    def __init__(self, t, buf, c0, width):
        self.t = t
        self.buf = buf
        self.c0 = c0
        self.width = width

    def __getitem__(self, idx):
        if not isinstance(idx, tuple):
            idx = (idx, slice(None))
        r, c = idx
        a = 0 if c.start is None else c.start
        b = self.width if c.stop is None else c.stop
        return View(self.t[r, self.c0 + a:self.c0 + b], (self.buf,))


def carve(tile, n, width):
    return [ColTile(tile.t, Buf(f"{tile.buf.name}_{i}"), i * width, width) for i in range(n)]


def _flat(x):
    out = []
    for i in x:
        if isinstance(i, View):
            out.extend(i.bufs)
        elif isinstance(i, Tile):
            out.append(i.buf)
        elif isinstance(i, Buf):
            out.append(i)
        elif i is None:
            continue
        elif isinstance(i, (list, tuple)):
            out.extend(_flat(i))
    return out


class Sched:
    def __init__(self, nc, stack):
        self.nc = nc
        self.stack = stack
        self.ops = {e: [] for e in ENGS}
        self.count = {e: 0 for e in ENGS}
        self.known = {e: {} for e in ENGS}
        self.sems = {}
        self.nsem = 0
        self.uid = 0
        self.gstack = stack
        self.free_dsems = {}
        self.phase_bufs = []
        self.phase_dma_evs = []
        self.rank_cache = {}

    def _name(self, name):
        self.uid += 1
        return f"{name}_{self.uid}"

    def sb(self, name, shape, dtype):
        t = self.stack.enter_context(self.nc.sbuf_tensor(self._name(name), list(shape), dtype))
        b = Buf(name)
        self.phase_bufs.append(b)
        return Tile(t, b)

    def ps(self, name, shape, dtype):
        t = self.stack.enter_context(self.nc.psum_tensor(self._name(name), list(shape), dtype))
        b = Buf(name)
        self.phase_bufs.append(b)
        return Tile(t, b)

    def dram(self, name, shape, dtype):
        t = self.nc.dram_tensor(name, list(shape), dtype)
        return Tile(t, Buf(name))

    def sem(self, key):
        if key not in self.sems:
            self.nsem += 1
            self.sems[key] = self.gstack.enter_context(self.nc.semaphore(f"s{self.nsem}"))
        return self.sems[key]

    def begin_phase(self):
        self.pstack = ExitStack()
        self.stack = self.pstack
        self.phase_bufs = []
        self.phase_dma_evs = []

    def end_phase(self):
        last = {}
        for (key, val) in self.phase_dma_evs:
            last[key] = max(last.get(key, 0), val)
        for ev in last.items():
            self.wait_event("sp", ev)
        self.emit()
        self.ops = {e: [] for e in ENGS}
        for b in self.phase_bufs:
            if b.dsem is not None:
                for cls, key in b.dsem.items():
                    self.free_dsems.setdefault(cls, []).append((key, b.dcount[cls]))
        self.phase_bufs = []
        self.pstack.close()
        self.stack = self.gstack

    def rank(self, e, name):
        if name not in self.rank_cache:
            self.rank_cache[name] = e.partition_id() % 4
        return self.rank_cache[name]

    def _deps(self, eng, reads, writes):
        evs = []
        for b in reads:
            if b.last_w is not None:
                evs.append(b.last_w)
        for b in writes:
            if b.last_w is not None:
                evs.append(b.last_w)
            evs.extend(b.readers)
        waits = {}
        kn = self.known[eng]
        for (key, val) in evs:
            if key[0] == "eng" and key[1] == eng and eng == "pe":
                continue
            if kn.get(key, 0) >= val:
                continue
            if waits.get(key, 0) < val:
                waits[key] = val
        for key, val in waits.items():
            kn[key] = val
        return list(waits.items())

    def _record(self, ev, reads, writes):
        for b in reads:
            b.readers.append(ev)
        for b in writes:
            b.last_w = ev
            b.readers = []

    def op(self, eng, fn, reads=(), writes=()):
        reads = _flat(reads)
        writes = _flat(writes)
        waits = self._deps(eng, reads, writes)
        self.count[eng] += 1
        if self.count[eng] % EPOCH == 0:
            self.count[eng] += 1
        c = self.count[eng]
        key = ("eng", eng, c // EPOCH)
        val = c % EPOCH
        ev = (key, val)
        self.ops[eng].append((waits, fn, key, 1))
        self._record(ev, reads, writes)
        return ev

    def dma(self, queue, out, in_, sem_buf=None, **kw):
        reads, writes = [], []
        if isinstance(in_, View):
            reads += list(in_.bufs)
            in_ap = in_.ap
        else:
            in_ap = in_
        if isinstance(out, View):
            writes += list(out.bufs)
            out_ap = out.ap
        else:
            out_ap = out
        if sem_buf is None:
            if writes:
                sem_buf = writes[0]
            elif reads:
                sem_buf = reads[0]
            else:
                sem_buf = Buf("dram2dram")
                self.phase_bufs.append(sem_buf)
        elif isinstance(sem_buf, Tile):
            sem_buf = sem_buf.buf
        waits = self._deps(queue, reads, writes)
        cls = "sw" if queue == "pool" else "hw"
        if sem_buf.dsem is None:
            sem_buf.dsem = {}
            sem_buf.dcount = {}
        if cls not in sem_buf.dsem:
            fl = self.free_dsems.setdefault(cls, [])
            if fl:
                sem_buf.dsem[cls], sem_buf.dcount[cls] = fl.pop()
            else:
                self.uid += 1
                sem_buf.dsem[cls] = ("dma", cls, self.uid)
                sem_buf.dcount[cls] = 0
        sem_buf.dcount[cls] += 16
        ev = (sem_buf.dsem[cls], sem_buf.dcount[cls])
        self.phase_dma_evs.append(ev)

        def fn(e, out_ap=out_ap, in_ap=in_ap, kw=kw, queue=queue):
            o = out_ap(self.rank(e, queue)) if callable(out_ap) else out_ap
            i = in_ap(self.rank(e, queue)) if callable(in_ap) else in_ap
            return e.dma_start(out=o, in_=i, **kw)

        self.ops[queue].append((waits, fn, sem_buf.dsem[cls], 16))
        self._record(ev, reads, writes)
        return ev

    def wait_event(self, eng, ev):
        key, val = ev
        if self.known[eng].get(key, 0) >= val:
            return
        self.known[eng][key] = val
        self.ops[eng].append(([(key, val)], None, None, 0))

    def emit(self):
        nc = self.nc
        for e in ENGS:
            for (waits, fn, key, inc) in self.ops[e]:
                for (k, v) in waits:
                    self.sem(k)
                if key is not None:
                    self.sem(key)
        engmap = {"pe": "tensor", "dve": "vector", "act": "scalar", "pool": "gpsimd", "sp": "sync"}
        with nc.Block() as block:
            for e in ENGS:
                ops = self.ops[e]

                def body(eng, ops=ops):
                    for (waits, fn, key, inc) in ops:
                        for (k, v) in waits:
                            eng.wait_ge(self.sems[k], v)
                        if fn is not None:
                            ins = fn(eng)
                            ins.then_inc(self.sems[key], inc)

                getattr(block, engmap[e])(body)


def _ap(x):
    return x.ap if isinstance(x, View) else x


def _tr(xs):
    return [x for x in xs if isinstance(x, (View, Tile))]


def mm(S, out, lhsT, rhs, start=True, stop=True, **kw):
    rd = [lhsT, rhs] + ([] if start else [out])
    return S.op("pe", lambda e: e.matmul(_ap(out), _ap(lhsT), _ap(rhs), start=start, stop=stop, **kw),
                reads=rd, writes=[out])


def transpose(S, out, in_, ident):
    return S.op("pe", lambda e: e.transpose(_ap(out), _ap(in_), _ap(ident)), reads=[in_, ident], writes=[out])


def act(S, out, in_, func, bias=None, scale=None, accum_out=None):
    kw = {}
    if bias is not None:
        kw["bias"] = _ap(bias)
    if scale is not None:
        kw["scale"] = _ap(scale)
    wr = [out]
    if accum_out is not None:
        kw["accum_out"] = _ap(accum_out)
        wr.append(accum_out)
    return S.op("act", lambda e: e.activation(_ap(out), _ap(in_), func, **kw), reads=_tr([in_, bias, scale]), writes=wr)


def tt(S, eng, out, in0, in1, op):
    return S.op(eng, lambda e: e.tensor_tensor(_ap(out), _ap(in0), _ap(in1), op), reads=[in0, in1], writes=[out])


def ts(S, eng, out, in0, s1, op0, s2=None, op1=None):
    kw = {}
    if op1 is not None:
        kw["op1"] = op1
    return S.op(eng, lambda e: e.tensor_scalar(_ap(out), _ap(in0), _ap(s1), _ap(s2) if s2 is not None else None, op0, **kw),
                reads=_tr([in0, s1, s2]), writes=[out])


def stt(S, eng, out, in0, scalar, in1, op0, op1):
    return S.op(eng, lambda e: e.scalar_tensor_tensor(_ap(out), _ap(in0), _ap(scalar), _ap(in1), op0, op1),
                reads=_tr([in0, scalar, in1]), writes=[out])


def copy(S, eng, out, in_):
    if eng == "act":
        return S.op(eng, lambda e: e.copy(_ap(out), _ap(in_)), reads=[in_], writes=[out])
    return S.op(eng, lambda e: e.tensor_copy(_ap(out), _ap(in_)), reads=[in_], writes=[out])


def memset(S, eng, out, val):
    return S.op(eng, lambda e: e.memset(_ap(out), val), reads=[], writes=[out])


def recip(S, out, in_):
    return S.op("dve", lambda e: e.reciprocal(_ap(out), _ap(in_)), reads=[in_], writes=[out])


class PsumPool:
    def __init__(self, S, n=8, shape=(128, 512), dtype=F32, name="ps"):
        self.tiles = [S.ps(f"{name}{i}", shape, dtype) for i in range(n)]
        self.i = 0

    def get(self):
        t = self.tiles[self.i % len(self.tiles)]
        self.i += 1
        return t


class Rot:
    def __init__(self, tiles):
        self.tiles = tiles
        self.i = 0

    def get(self):
        t = self.tiles[self.i % len(self.tiles)]
        self.i += 1
        return t


def new_nc():
    return bass.Bass("TRN2", target_bir_lowering=False)


def w_view(w_ap):
    return w_ap.rearrange("(kc p) n -> p kc n", p=128)


class Cfg:
    def __init__(self, T_):
        self.T = T_
        self.TL = T_ // 4
        self.TC = CTX // 4
        self.NT = self.TL + self.TC
        self.NTOK = CTX + T_
        tt_ = [(t0, min(512, self.TL - t0)) for t0 in range(0, self.TL, 512)]
        self.TOK_TILES = tt_ + [(self.TL, self.TC)]
        self.SEQ_TILES = [(0, CTX, 0, CTX)] + [(CTX + i * 512, 512, CTX, self.NTOK) for i in range(T_ // 512)]


GROUPS = [[0, 1, 2, 3], [4, 5, 6, 7]]
KINDS = [0, 1, 2, 0]
N_INS = [4128, 3072, 3072, 4128]
BIG = 30000.0
CM_ID, CM_ONES, CM_MF, CM_MB, CM_BS0, CM_BS1, CM_POSF, CM_NEGF, CM_POSB, CM_NEGB = range(10)
CM_MU = 10
CM_ML = 16
CM_SWAP = 22
NCM = 23
DIFF_LAYER = 2
LAMBDA_INIT = 0.8 - 0.6 * math.exp(-0.3 * DIFF_LAYER)


def consts_cm():
    i = np.arange(128)[:, None]
    j = np.arange(128)[None, :]
    same = (i // 64) == (j // 64)
    cm = np.zeros((NCM, 128, 128), np.float32)
    cm[CM_ID] = (i == j)
    cm[CM_ONES] = 1.0
    cm[CM_MF] = same & (i <= j)
    cm[CM_MB] = same & (i >= j)
    cm[CM_BS0] = np.broadcast_to(i < 64, (128, 128))
    cm[CM_BS1] = np.broadcast_to(i >= 64, (128, 128))
    cm[CM_POSF] = np.where(same & (j < i), 0.0, BIG)
    cm[CM_NEGF] = np.where(same & (i <= j), 0.0, -BIG)
    cm[CM_POSB] = np.where(same & (j > i), 0.0, BIG)
    cm[CM_NEGB] = np.where(same & (i >= j), 0.0, -BIG)
    for l in range(6):
        sz = 1 << l
        mu = ((i // (2 * sz)) == (j // (2 * sz))) & ((i % (2 * sz)) < sz) & ((j % (2 * sz)) >= sz)
        cm[CM_MU + l] = mu
        cm[CM_ML + l] = mu.T
    cm[CM_SWAP] = (i == (j ^ 1))
    return cm


def rope_tables(tl):
    n_rows = tl // 64
    row = np.repeat(np.arange(n_rows), 64).astype(np.float32)
    col = np.tile(np.arange(64), n_rows).astype(np.float32)
    nf = 16
    inv = (np.float32(10000.0) ** (-np.arange(nf, dtype=np.float32) / nf)).astype(np.float32)
    ang = np.concatenate([row[:, None] * inv, col[:, None] * inv], axis=-1).astype(np.float32)
    cos = np.cos(ang).astype(np.float32)
    sin = np.sin(ang).astype(np.float32)
    cs = np.zeros((2, 128, tl), np.float32)
    for m in range(2):
        for d in range(64):
            i = d // 2
            cs[0, m * 64 + d] = cos[:, i]
            cs[1, m * 64 + d] = -sin[:, i] if d % 2 == 0 else sin[:, i]
    return cs


def wait_all_dma(S, engs=("sp", "act", "pool")):
    last = {}
    for (key, val) in S.phase_dma_evs:
        last[key] = max(last.get(key, 0), val)
    for q in engs:
        for ev in last.items():
            S.wait_event(q, ev)


def all_gather(S, src, dst):
    wait_all_dma(S, ("pool",))
    ev = S.op("pool", lambda e: e.collective_compute("AllGather", ALU.bypass, replica_groups=GROUPS,
                                                     ins=[src.ap().opt()], outs=[dst.ap().opt()]))
    for q in ENGS:
        S.wait_event(q, ev)
    return ev


def localize(S, q, gath, loc, nparts, C, hp=256):
    g = gath.ap()[:, 0:nparts * 4 * hp, :].rearrange("s (xi r hp) c -> s xi r hp c", xi=nparts, r=4)
    l = loc.ap()[0:nparts]
    S.dma(q, l[:, :, CTX:].rearrange("xi hp (s t) -> s xi hp t", s=4),
          lambda R: g[:, :, bass.ds(R, 1), :, 0:C.TL].rearrange("s xi r hp t -> s xi (r hp) t"))
    S.dma(q, l[:, :, 0:CTX].rearrange("xi hp (s t) -> s xi hp t", s=4),
          lambda R: g[:, :, bass.ds(R, 1), :, C.TL:C.NT].rearrange("s xi r hp t -> s xi (r hp) t"))


def phase_M(S, io, C):
    S.begin_phase()
    nch = 48
    ct = S.sb("ct", [128, 16], F32)
    sg = S.sb("sg", [128, 16], F32)
    sT = S.sb("sT", [128, 8, 2], BF16)
    bt = S.sb("bt", [128, nch], F32)
    res = S.sb("res", [128, nch, 2], F32)
    S.dma("sp", ct[:], io["cT"].ap())
    S.dma("sp", bt[:], io["mb"].ap())
    wv = w_view(io["mw"].ap())
    wts = Rot([S.sb(f"w{j}", [128, 8, 1024], BF16) for j in range(3)])
    act(S, sg[:], ct[:], AF.Sigmoid)
    tt(S, "dve", sT.v(sT.t[:].rearrange("p a b -> p (a b)")), ct[:], sg[:], ALU.mult)
    pp = PsumPool(S, 4, (128, 512), F32)
    for blk in range(6):
        wt = wts.get()
        S.dma("pool", wt[:], wv[:, :, blk * 1024:(blk + 1) * 1024], max_dma_last_dim=4096)
        for jj in range(8):
            j = blk * 8 + jj
            ps = pp.get()
            for kc in range(8):
                mm(S, ps[:, 0:2], wt[:, kc, jj * 128:(jj + 1) * 128], sT[:, kc, :], start=(kc == 0), stop=(kc == 7))
            act(S, res[:, j, :], ps[:, 0:2], AF.Identity, bias=bt[:, j:j + 1])
    S.dma("sp", io["modb"].ap(), res.v(res.t[:].rearrange("p a b -> p (a b)")))
    all_gather(S, io["modb"], io["modg"])
    S.end_phase()


def norm_mod(S, pp, ones_bf, xt, ht, w, gs, sh, tmp_sq, tmp_r, tmp_x, eps_t):
    for kc in range(8):
        act(S, tmp_sq[:, kc, :w], xt[:, kc, :w], AF.Square)
    ps = pp.get()
    for kc in range(8):
        mm(S, ps[:, :w], ones_bf[:], tmp_sq[:, kc, :w], start=(kc == 0), stop=(kc == 7))
    act(S, tmp_r[:, :w], ps[:, :w], AF.Sqrt, bias=eps_t[:, 0:1], scale=1.0 / D)
    recip(S, tmp_r[:, :w], tmp_r[:, :w])
    for kc in range(8):
        tx = tmp_x.get()
        tt(S, "dve" if kc % 2 == 0 else "pool", tx[:, :w], xt[:, kc, :w], tmp_r[:, :w], ALU.mult)
        act(S, ht[:, kc, :w], tx[:, :w], AF.Identity, bias=sh[:, kc:kc + 1], scale=gs[:, kc:kc + 1])


def phase_T(S, io, C, i):
    has_prev = i > 0
    final = i == DEPTH
    n_in = 0 if final else N_INS[i]
    NT, TL, TC = C.NT, C.TL, C.TC
    TOK_TILES = C.TOK_TILES
    S.begin_phase()
    pp = PsumPool(S, 8)
    nv = S.sb("nv", [128, 9, 8], F32)
    S.dma("sp", nv.v(nv.t[:].rearrange("p a b -> p (a b)")), io["nvec"].ap())
    ones_bf = S.sb("ones", [128, 128], BF16)
    memset(S, "dve", ones_bf[:], 1.0)
    eps_t = S.sb("eps", [128, 1], F32)
    memset(S, "dve", eps_t[:], EPS)
    modP = modN = None
    if has_prev:
        modP = S.sb("modP", [128, 48, 2], F32)
        S.dma("act", modP.v(modP.t[:].rearrange("p a b -> p (a b)")), io["modg"].ap()[i - 1])
    if not final:
        modN = S.sb("modN", [128, 48, 2], F32)
        S.dma("act", modN.v(modN.t[:].rearrange("p a b -> p (a b)")), io["modg"].ap()[i])
    mv = lambda m, j, row: m[:, j * 8:(j + 1) * 8, row]
    xts = [S.sb(f"x{k}", [128, 8, w], F32) for k, (t0, w) in enumerate(TOK_TILES)]
    hts = [S.sb(f"h{k}", [128, 8, w], BF16) for k, (t0, w) in enumerate(TOK_TILES)]
    tmp_sq = S.sb("tsq", [128, 8, 512], BF16)
    tmp_r = S.sb("tr", [128, 512], F32)
    scr = Rot([S.sb(f"scr{k}", [128, 512], F32) for k in range(5)])
    stg = Rot([S.sb(f"stg{k}", [128, 512], BF16) for k in range(3)])
    wblk = Rot([S.sb(f"wblk{k}", [128, 8, 512], BF16) for k in range(2)])
    xsrc = io["xT"] if i <= 1 else io["xbuf"]
    xv = w_view(xsrc.ap())
    for k, (t0, w) in enumerate(TOK_TILES):
        S.dma("sp" if k % 2 == 0 else "act", xts[k][:], xv[:, :, t0:t0 + w])
    gsl = S.sb("gsl", [128, 8], F32)
    gsc = S.sb("gsc", [128, 8], F32)
    is_ctx = lambda k: k == len(TOK_TILES) - 1

    if has_prev:
        L = i - 1
        oh = S.sb("oh", [128, 4], F32)
        S.dma("sp", oh[:], io["oh"].ap())
        ogv = [g_.ap().rearrange("s (kh p) c -> p (s kh) c", kh=2) for g_ in io["og"]]
        cands = Rot([S.sb(f"cand{k}", [128, 8, 512], BF16) for k in range(2)])
        for k, (t0, w) in enumerate(TOK_TILES):
            for rr in range(4):
                cnd = cands.get()
                if is_ctx(k):
                    src_ = ogv[0][:, :, rr * TC:(rr + 1) * TC]
                else:
                    src_ = ogv[1 + (rr * TL + t0) // 512][:, :, 0:w]
                S.dma("sp" if rr % 2 == 0 else "act", cnd[:, :, :w], src_)
                if rr == 0:
                    ts(S, "dve", hts[k][:, :, :w], cnd[:, :, :w], oh[:, 0:1], ALU.mult)
                else:
                    stt(S, "dve", hts[k][:, :, :w], cnd[:, :, :w], oh[:, rr:rr + 1], hts[k][:, :, :w], ALU.mult, ALU.add)
        wo = S.sb("wo", [128, 8, D], BF16)
        S.dma("pool", wo[:], w_view(io["w_out"][L].ap()), max_dma_last_dim=4096)
        for k, (t0, w) in enumerate(TOK_TILES):
            g1 = mv(modP, 2, 1 if is_ctx(k) else 0)
            for dc in range(8):
                ps = pp.get()
                for kc in range(8):
                    mm(S, ps[:, :w], wo[:, kc, dc * 128:(dc + 1) * 128], hts[k][:, kc, :w], start=(kc == 0), stop=(kc == 7))
                stt(S, "dve", xts[k][:, dc, :w], ps[:, :w], g1[:, dc:dc + 1], xts[k][:, dc, :w], ALU.mult, ALU.add)
        stt(S, "dve", gsl[:], mv(modP, 4, 0), 1.0, nv[:, 4 + L, :], ALU.add, ALU.mult)
        stt(S, "dve", gsc[:], mv(modP, 4, 1), 1.0, nv[:, 4 + L, :], ALU.add, ALU.mult)
        for k, (t0, w) in enumerate(TOK_TILES):
            row = 1 if is_ctx(k) else 0
            norm_mod(S, pp, ones_bf, xts[k], hts[k], w, gsc if row else gsl, mv(modP, 3, row), tmp_sq, tmp_r, scr, eps_t)
        w2b = Rot([S.sb(f"w2b{k}", [128, 4, D], BF16) for k in range(2)])
        uts = [S.sb(f"u{k}", [128, 4, w], BF16) for k, (t0, w) in enumerate(TOK_TILES)]
        w1v = w_view(io["w1"][L].ap())
        w2v = w_view(io["w2"][L].ap())
        for fb in range(8):
            a1 = wblk.get()
            a2 = w2b.get()
            S.dma("pool", a1[:], w1v[:, :, fb * 512:(fb + 1) * 512], max_dma_last_dim=2048)
            S.dma("pool", a2[:], w2v[:, fb * 4:(fb + 1) * 4, :], max_dma_last_dim=4096)
            for k, (t0, w) in enumerate(TOK_TILES):
                g2 = mv(modP, 5, 1 if is_ctx(k) else 0)
                for fc in range(4):
                    ps = pp.get()
                    for kc in range(8):
                        mm(S, ps[:, :w], a1[:, kc, fc * 128:(fc + 1) * 128], hts[k][:, kc, :w], start=(kc == 0), stop=(kc == 7))
                    r = scr.get()
                    act(S, r[:, :w], ps[:, :w], AF.Relu)
                    tt(S, "pool", uts[k][:, fc, :w], r[:, :w], r[:, :w], ALU.mult)
                for dc in range(8):
                    ps = pp.get()
                    for fc in range(4):
                        mm(S, ps[:, :w], a2[:, fc, dc * 128:(dc + 1) * 128], uts[k][:, fc, :w], start=(fc == 0), stop=(fc == 3))
                    stt(S, "dve", xts[k][:, dc, :w], ps[:, :w], g2[:, dc:dc + 1], xts[k][:, dc, :w], ALU.mult, ALU.add)
        if not final:
            xov = w_view(io["xbuf"].ap())
            for k, (t0, w) in enumerate(TOK_TILES):
                S.dma("sp", xov[:, :, t0:t0 + w], xts[k][:])

    if not final:
        stt(S, "dve", gsl[:], mv(modN, 1, 0), 1.0, nv[:, i, :], ALU.add, ALU.mult)
        stt(S, "dve", gsc[:], mv(modN, 1, 1), 1.0, nv[:, i, :], ALU.add, ALU.mult)
        for k, (t0, w) in enumerate(TOK_TILES):
            row = 1 if is_ctx(k) else 0
            norm_mod(S, pp, ones_bf, xts[k], hts[k], w, gsc if row else gsl, mv(modN, 0, row), tmp_sq, tmp_r, scr, eps_t)
        for k, (t0, w) in enumerate(TOK_TILES):
            for kc in range(8):
                S.dma("sp" if kc % 2 == 0 else "act", io["hb"][kc].ap()[:, t0:t0 + w], hts[k][:, kc, :w])
        for kc in range(8):
            all_gather(S, io["hb"][kc], io["hg"][kc])
    else:
        yv = w_view(io["yo"].ap())
        for k, (t0, w) in enumerate(TOK_TILES[:-1]):
            xt = xts[k]
            for kc in range(8):
                act(S, tmp_sq[:, kc, :w], xt[:, kc, :w], AF.Square)
            ps = pp.get()
            for kc in range(8):
                mm(S, ps[:, :w], ones_bf[:], tmp_sq[:, kc, :w], start=(kc == 0), stop=(kc == 7))
            act(S, tmp_r[:, :w], ps[:, :w], AF.Sqrt, bias=eps_t[:, 0:1], scale=1.0 / D)
            recip(S, tmp_r[:, :w], tmp_r[:, :w])
            for kc in range(8):
                stt(S, "dve", xt[:, kc, :w], xt[:, kc, :w], nv[:, 8, kc:kc + 1], tmp_r[:, :w], ALU.mult, ALU.mult)
            S.dma("sp", yv[:, :, t0:t0 + w], xt[:])
    S.end_phase()


def phase_inproj(S, io, C, i):
    kind = KINDS[i]
    nmain = 1024 if kind == 0 else 768
    ncol = nmain + (8 if kind == 0 else 0)
    S.begin_phase()
    pp = PsumPool(S, 8)
    wsb = S.sb("wl", [128, 8, ncol], BF16)
    wv = w_view(io["wl"][i].ap())
    for c0 in range(0, ncol, 512):
        cw = min(512, ncol - c0)
        S.dma("pool", wsb[:, :, c0:c0 + cw], wv[:, :, c0:c0 + cw], max_dma_last_dim=2048)
    hts = Rot([S.sb(f"ht{k}", [128, 8, 512], BF16) for k in range(3)])
    stg = Rot([S.sb(f"stg{k}", [128, 512], BF16) for k in range(4)])
    stf = Rot([S.sb(f"stf{k}", [8, 512], F32) for k in range(2)])
    zl = io["zl"].ap()
    zabl = io["zabl"].ap().rearrange("kd hh c -> (kd hh) c")
    hg = [g_.ap() for g_ in io["hg"]]
    kk = 0
    for (t0, w, s0, s1) in C.SEQ_TILES:
        ht = hts.get()
        if t0 < CTX:
            for s_ in range(4):
                for kc in range(8):
                    S.dma("sp" if kc % 2 == 0 else "act", ht[:, kc, s_ * C.TC:(s_ + 1) * C.TC], hg[kc][s_, :, C.TL:C.NT])
        else:
            s_, c0 = divmod(t0 - CTX, C.TL)
            for kc in range(8):
                S.dma("sp" if kc % 2 == 0 else "act", ht[:, kc, :w], hg[kc][s_, :, c0:c0 + w])
        for m in range(nmain // 128):
            xi, hh = divmod(m, 2)
            ps = pp.get()
            for kc in range(8):
                mm(S, ps[:, :w], wsb[:, kc, m * 128:(m + 1) * 128], ht[:, kc, :w], start=(kc == 0), stop=(kc == 7))
            sg_ = stg.get()
            copy(S, "act" if kk % 2 == 0 else "dve", sg_[:, :w], ps[:, :w])
            S.dma("sp" if kk % 2 == 0 else "act", zl[xi, hh * 128:(hh + 1) * 128, t0:t0 + w], sg_[:, :w])
            kk += 1
        if kind == 0:
            ps = pp.get()
            for kc in range(8):
                mm(S, ps[:8, :w], wsb[:, kc, 1024:1032], ht[:, kc, :w], start=(kc == 0), stop=(kc == 7))
            sf = stf.get()
            copy(S, "dve", sf[:, :w], ps[:8, :w])
            S.dma("sp", zabl[:, t0:t0 + w], sf[:, :w])
    S.end_phase()


def phase_H_sconv(S, io, C):
    S.begin_phase()
    zl = io["zl"].ap()
    cwt = S.sb("cw", [128, 2, 3], F32)
    S.dma("sp", cwt.v(cwt.t[:].rearrange("p a b -> p (a b)")), io["scw"].ap())
    NB = 3
    bgs = Rot([S.sb(f"bg{k}", [128, 512], BF16) for k in range(NB)])
    cgs = Rot([S.sb(f"cg{k}", [128, 514], BF16) for k in range(NB)])
    hvs = Rot([S.sb(f"hv{k}", [128, 514], BF16) for k in range(NB)])
    us = Rot([S.sb(f"u{k}", [128, 514], F32) for k in range(NB)])
    accs = Rot([S.sb(f"acc{k}", [128, 512], F32) for k in range(NB)])
    outs = Rot([S.sb(f"o{k}", [128, 512], BF16) for k in range(NB)])
    k = 0
    for j in range(2):
        rows = slice(j * 128, (j + 1) * 128)
        for ti, (t0, w, s0, s1) in enumerate(C.SEQ_TILES):
            bg, cg, hv, u, acc, o = bgs.get(), cgs.get(), hvs.get(), us.get(), accs.get(), outs.get()
            lo = 1 if t0 == s0 else 0
            hi = 1 if t0 + w == s1 else 0
            q1, q2 = ("sp", "act") if k % 2 == 0 else ("act", "sp")
            S.dma(q1, bg[:, :w], zl[0, rows, t0:t0 + w])
            if lo:
                memset(S, "pool", cg[:, 0:1], 0.0)
                memset(S, "pool", hv[:, 0:1], 0.0)
            if hi:
                memset(S, "pool", cg[:, w + 1:w + 2], 0.0)
                memset(S, "pool", hv[:, w + 1:w + 2], 0.0)
            S.dma(q2, cg[:, lo:w + 2 - hi], zl[1, rows, t0 - 1 + lo:t0 + w + 1 - hi])
            S.dma(q1, hv[:, lo:w + 2 - hi], zl[2, rows, t0 - 1 + lo:t0 + w + 1 - hi])
            tt(S, "pool", u[:, :w + 2], cg[:, :w + 2], hv[:, :w + 2], ALU.mult)
            ts(S, "dve", acc[:, :w], u[:, 0:w], cwt[:, j, 0:1], ALU.mult)
            stt(S, "dve", acc[:, :w], u[:, 1:w + 1], cwt[:, j, 1:2], acc[:, :w], ALU.mult, ALU.add)
            stt(S, "dve", acc[:, :w], u[:, 2:w + 2], cwt[:, j, 2:3], acc[:, :w], ALU.mult, ALU.add)
            tt(S, "pool", o[:, :w], acc[:, :w], bg[:, :w], ALU.mult)
            S.dma(q2, io["ob"][ti].ap()[rows, 0:w], o[:, :w])
            k += 1
    for ti in range(len(C.SEQ_TILES)):
        all_gather(S, io["ob"][ti], io["og"][ti])
    S.end_phase()


def phase_H_diff(S, io, C):
    tl = C.T
    ntok = C.NTOK
    nkt = ntok // 128
    S.begin_phase()
    zl = io["zl"].ap()
    cs = io["cs"].ap()
    ones_bf = S.sb("ones", [128, 128], BF16)
    memset(S, "dve", ones_bf[:], 1.0)
    eps_t = S.sb("eps", [128, 1], F32)
    memset(S, "dve", eps_t[:], EPS)
    cmf = S.sb("cmf", [2, 128, 128], F32) if False else None
    idf = S.sb("idf", [128, 128], F32)
    swf = S.sb("swf", [128, 128], F32)
    S.dma("sp", idf[:], io["cm"].ap()[CM_ID])
    S.dma("act", swf[:], io["cm"].ap()[CM_SWAP])
    id_bf = S.sb("idbf", [128, 128], BF16)
    sw_bf = S.sb("swbf", [128, 128], BF16)
    copy(S, "dve", id_bf[:], idf[:])
    copy(S, "dve", sw_bf[:], swf[:])
    lamt = S.sb("lam", [128, 4, 64], F32)
    S.dma("sp", lamt.v(lamt.t[:].rearrange("p a b -> p (a b)")), io["dlam"].ap())
    ngt = S.sb("ng", [128, 1], F32)
    S.dma("sp", ngt[:], io["dng"].ap())
    lp = S.sb("lp", [128, 2, 64], F32)
    ls = S.sb("ls", [128, 2], F32)
    tt(S, "dve", lp[:, 0, :], lamt[:, 0, :], lamt[:, 1, :], ALU.mult)
    tt(S, "dve", lp[:, 1, :], lamt[:, 2, :], lamt[:, 3, :], ALU.mult)
    S.op("dve", lambda e: e.tensor_reduce(ls.t[:, :], lp.t[:, :, :], AX.X, ALU.add), reads=[lp], writes=[ls])
    act(S, ls[:], ls[:], AF.Exp)
    nlmb = S.sb("nlmb", [128, 1], F32)
    tt(S, "dve", nlmb[:], ls[:, 1:2], ls[:, 0:1], ALU.subtract)
    ts(S, "dve", nlmb[:], nlmb[:], -LAMBDA_INIT, ALU.add)
    gsc = S.sb("gsc", [128, 1], F32)
    ts(S, "dve", gsc[:], ngt[:], 1.0 - LAMBDA_INIT, ALU.mult)

    KT = S.sb("KT", [128, ntok], BF16)
    QT = S.sb("QT", [128, ntok], BF16)
    VTf = S.sb("VTf", [128, ntok], BF16)
    V = S.sb("V", [128, nkt, 128], BF16)
    ldc = Rot([S.sb(f"ldc{k}", [128, 512], F32) for k in range(4)])
    ldb = Rot([S.sb(f"ldb{k}", [128, 512], BF16) for k in range(4)])
    t1s = Rot([S.sb(f"t1{k}", [128, 512], F32) for k in range(4)])
    Pt = Rot([S.sb(f"P{k}", [128, 1024], BF16) for k in range(3)])
    psS = Rot([S.ps(f"psS{k}", [128, 1024], F32) for k in range(2)])
    psO = [S.ps(f"psO{k}", [128, 512], F32) for k in range(2)]
    psL = [S.ps(f"psL{k}", [128, 512], F32) for k in range(2)]
    fin = Rot([S.sb(f"fin{k}", [128, 512], F32) for k in range(6)])
    ost = Rot([S.sb(f"ost{k}", [128, 512], BF16) for k in range(2)])
    sqb = S.sb("sqb", [128, 512], BF16)

    for hh in range(2):
        rows = slice(hh * 128, (hh + 1) * 128)
        S.dma("sp", QT[:, 0:CTX], zl[0, rows, 0:CTX])
        S.dma("act", KT[:, 0:CTX], zl[1, rows, 0:CTX])
        S.dma("sp", VTf[:, :], zl[2, rows, :])
        trb = Rot([psO[0], psO[1], psL[0], psL[1]])
        for kt in range(nkt):
            pt = trb.get()
            ptv = pt.v(pt.t[:, 0:64].bitcast(BF16))
            transpose(S, ptv, VTf[:, kt * 128:(kt + 1) * 128], id_bf[:])
            copy(S, "act" if kt % 2 == 0 else "dve", V[:, kt, :], ptv)
        for ti in range(tl // 512):
            c0 = ti * 512
            cosb, sinb = ldc.get(), ldc.get()
            S.dma("sp", cosb[:], cs[0, :, c0:c0 + 512])
            S.dma("act", sinb[:], cs[1, :, c0:c0 + 512])
            for (part, dst) in ((0, QT), (1, KT)):
                a = ldb.get()
                S.dma("sp" if part == 0 else "act", a[:], zl[part, rows, CTX + c0:CTX + c0 + 512])
                psw = psS.get()
                mm(S, psw[:, 0:512], sw_bf[:], a[:])
                t1, t2 = t1s.get(), t1s.get()
                tt(S, "pool", t1[:], a[:], cosb[:], ALU.mult)
                tt(S, "dve", t2[:], psw[:, 0:512], sinb[:], ALU.mult)
                tt(S, "dve", dst[:, CTX + c0:CTX + c0 + 512], t1[:], t2[:], ALU.add)
        qtiles = [(0, CTX, CTX // 128)] + [(CTX + k * 512, 512, nkt) for k in range(tl // 512)]
        for qi, (q0, qw, nk) in enumerate(qtiles):
            def qk(kt):
                ps = psS.get()
                for m in range(2):
                    mm(S, ps[:, m * 512:m * 512 + qw], KT[m * 64:(m + 1) * 64, kt * 128:(kt + 1) * 128],
                       QT[m * 64:(m + 1) * 64, q0:q0 + qw])
                p = Pt.get()
                if qw == 512:
                    act(S, p[:, :], ps[:, :], AF.Exp, scale=0.125)
                else:
                    for m in range(2):
                        act(S, p[:, m * 512:m * 512 + qw], ps[:, m * 512:m * 512 + qw], AF.Exp, scale=0.125)
                return p

            def av(kt, p):
                for m in range(2):
                    mm(S, psO[m][:, :qw], V[:, kt, :], p[:, m * 512:m * 512 + qw], start=(kt == 0), stop=(kt == nk - 1))
                    mm(S, psL[m][:, :qw], ones_bf[:], p[:, m * 512:m * 512 + qw], start=(kt == 0), stop=(kt == nk - 1))

            prev = None
            for kt in range(nk + 1):
                cur = qk(kt) if kt < nk else None
                if prev is not None:
                    av(kt - 1, prev)
                prev = cur
            r1, r2, a1, a2 = fin.get(), fin.get(), fin.get(), fin.get()
            recip(S, r1[:, :qw], psL[0][:, :qw])
            recip(S, r2[:, :qw], psL[1][:, :qw])
            tt(S, "dve", a1[:, :qw], psO[0][:, :qw], r1[:, :qw], ALU.mult)
            tt(S, "dve", a2[:, :qw], psO[1][:, :qw], r2[:, :qw], ALU.mult)
            stt(S, "dve", a1[:, :qw], a2[:, :qw], nlmb[:, 0:1], a1[:, :qw], ALU.mult, ALU.add)
            act(S, sqb[:, :qw], a1[:, :qw], AF.Square)
            pss = psS.get()
            mm(S, pss[:, :qw], ones_bf[:], sqb[:, :qw])
            act(S, r1[:, :qw], pss[:, :qw], AF.Sqrt, bias=eps_t[:, 0:1], scale=1.0 / 128)
            recip(S, r1[:, :qw], r1[:, :qw])
            o = ost.get()
            stt(S, "dve", o[:, :qw], a1[:, :qw], gsc[:, 0:1], r1[:, :qw], ALU.mult, ALU.mult)
            S.dma("sp", io["ob"][qi].ap()[rows, 0:qw], o[:, :qw])
    for ti in range(len(C.SEQ_TILES)):
        all_gather(S, io["ob"][ti], io["og"][ti])
    S.end_phase()


def phase_H_gdn(S, io, C, j):
    tl = C.T
    ntok = C.NTOK
    ntile = ntok // 128
    seqs = [(0, CTX), (CTX, ntok)]
    if True:
        S.begin_phase()
        zl = io["zl"].ap()
        zabl = io["zabl"].ap()
        cmin = io["cm"].ap()
        hc = io["ghc"].ap()[j]
        cm = [S.sb(f"cm{i}", [128, 128], F32) for i in range(NCM)]
        for i in range(NCM):
            S.dma("sp" if i % 2 == 0 else "act", cm[i][:], cmin[i])
        id_bf = S.sb("idbf", [128, 128], BF16)
        copy(S, "dve", id_bf[:], cm[CM_ID][:])
        ones_bf = S.sb("onesbf", [128, 128], BF16)
        memset(S, "dve", ones_bf[:], 1.0)
        lvl_bf = []
        for i in range(12):
            t_ = S.sb(f"lvl{i}", [128, 128], BF16)
            copy(S, "dve" if i % 2 == 0 else "pool", t_[:], cm[CM_MU + i][:])
            lvl_bf.append(t_)
        MU_bf, ML_bf = lvl_bf[0:6], lvl_bf[6:12]
        eps_t = S.sb("eps", [128, 1], F32)
        memset(S, "dve", eps_t[:], EPS)
        one_t = S.sb("one", [128, 1], F32)
        memset(S, "dve", one_t[:], 1.0)

        QT = S.sb("QT", [128, ntok], BF16)
        KT = S.sb("KT", [128, ntok], BF16)
        VT = S.sb("VT", [128, ntok], BF16)
        OACC = S.sb("OACC", [128, ntok], F32)
        hct = S.sb("hc", [128, 16], F32)
        abt = S.sb("ab", [128, ntile, 4], F32)
        abrow = S.sb("abrow", [4, ntok], F32)
        G = S.sb("G", [128, 2, ntile], F32)
        BETA = S.sb("BETA", [128, 2, ntile], F32)
        CUM = S.sb("CUM", [128, 2, ntile], F32)
        KBS = S.sb("KBS", [128, 2, ntile], F32)
        KTS = S.sb("KTS", [128, 2, ntile], F32)
        CD = S.sb("CD", [128, 2, 2, ntile], F32)
        nea = S.sb("nea", [128, 2], F32)

        ld = Rot([S.sb(f"ld{i}", [128, 514], BF16) for i in range(4)])
        ost = Rot([S.sb(f"ost{i}", [128, 512], BF16) for i in range(2)])
        cacc = Rot([S.sb(f"cacc{i}", [128, 512], F32) for i in range(4)])
        sqb = Rot([S.sb(f"sqb{i}", [128, 512], BF16) for i in range(2)])
        banks = [S.ps(f"bank{i}", [128, 512], F32) for i in range(7)]
        ps512 = Rot(banks)
        psg = Rot([ColTile(b.t, b.buf, 0, 128) for b in banks])
        tb = S.ps("pstb", [128, 1024], BF16)
        pst = Rot([ColTile(tb.t, tb.buf, i * 128, 128) for i in range(4)])
        f128 = Rot([S.sb(f"f128_{i}", [128, 128], F32) for i in range(8)])
        b128 = Rot([S.sb(f"b128_{i}", [128, 128], BF16) for i in range(52)])
        Sf = S.sb("Sf", [128, 128], F32)
        Sb = Rot([S.sb(f"Sb{i}", [128, 128], BF16) for i in range(2)])
        vn = S.sb("vn", [128, 128], BF16)
        memset(S, "dve", vn[:], 0.0)

        for hh in range(2):
            S.dma("sp", hct[:], hc[hh])
            S.dma("act", abrow[:, :], zabl[:, hh, :])
            for ti in range(ntile):
                pt_ = psg.get()
                transpose(S, pt_[:, 0:4], abrow[0:4, ti * 128:(ti + 1) * 128], cm[CM_ID][0:4, 0:4])
                copy(S, "act" if ti % 2 == 0 else "dve", abt[:, ti, :], pt_[:, 0:4])
            for xi, dst in ((0, QT), (1, KT), (2, VT)):
                for (s0, s1) in seqs:
                    for t0 in range(s0, s1, 512):
                        w = min(512, s1 - t0)
                        buf = ld.get()
                        lo = 1 if t0 == s0 else 0
                        hi = 1 if t0 + w == s1 else 0
                        if lo:
                            memset(S, "pool", buf[:, 0:1], 0.0)
                        if hi:
                            memset(S, "pool", buf[:, w + 1:w + 2], 0.0)
                        S.dma("sp", buf[:, lo:w + 2 - hi], zl[xi, hh * 128:(hh + 1) * 128, t0 - 1 + lo:t0 + w + 1 - hi])
                        acc = cacc.get()
                        ts(S, "pool", acc[:, :w], buf[:, 0:w], hct[:, 3 * xi:3 * xi + 1], ALU.mult)
                        stt(S, "dve", acc[:, :w], buf[:, 1:w + 1], hct[:, 3 * xi + 1:3 * xi + 2], acc[:, :w], ALU.mult, ALU.add)
                        stt(S, "dve", acc[:, :w], buf[:, 2:w + 2], hct[:, 3 * xi + 2:3 * xi + 3], acc[:, :w], ALU.mult, ALU.add)
                        if xi == 2:
                            act(S, dst[:, t0:t0 + w], acc[:, :w], AF.Silu)
                        else:
                            act(S, acc[:, :w], acc[:, :w], AF.Silu)
                            sq = sqb.get()
                            tt(S, "pool", sq[:, :w], acc[:, :w], acc[:, :w], ALU.mult)
                            ps = ps512.get()
                            mm(S, ps[:, :w], ones_bf[:], sq[:, :w])
                            r = cacc.get()
                            act(S, r[:, :w], ps[:, :w], AF.Sqrt, bias=eps_t[:, 0:1], scale=1.0)
                            recip(S, r[:, :w], r[:, :w])
                            if xi == 0:
                                stt(S, "dve", dst[:, t0:t0 + w], acc[:, :w], 128.0 ** -0.5, r[:, :w], ALU.mult, ALU.mult)
                            else:
                                tt(S, "dve", dst[:, t0:t0 + w], acc[:, :w], r[:, :w], ALU.mult)
            act(S, nea[:], hct[:, 9:11], AF.Exp)
            ts(S, "dve", nea[:], nea[:], -1.0, ALU.mult)
            for d in range(2):
                act(S, G[:, d, :], abt[:, :, d], AF.Exp, bias=hct[:, 11 + d:12 + d])
                act(S, G[:, d, :], G[:, d, :], AF.Ln, bias=one_t[:, 0:1])
                ts(S, "dve", G[:, d, :], G[:, d, :], nea[:, d:d + 1], ALU.mult)
                act(S, BETA[:, d, :], abt[:, :, 2 + d], AF.Sigmoid)
            for d in range(2):
                ps = ps512.get()
                mm(S, ps[:, 0:ntile], cm[CM_MF if d == 0 else CM_MB][:], G[:, d, :])
                mm(S, ps[:, 128:128 + ntile], cm[CM_BS0][:], G[:, d, :])
                mm(S, ps[:, 256:256 + ntile], cm[CM_BS1][:], G[:, d, :])
                copy(S, "dve", CUM[:, d, :], ps[:, 0:ntile])
                act(S, CD[:, d, 0, :], ps[:, 128:128 + ntile], AF.Exp)
                act(S, CD[:, d, 1, :], ps[:, 256:256 + ntile], AF.Exp)
                tmp = f128.get()
                tt(S, "dve", tmp[0:64, 0:ntile], ps[0:64, 128:128 + ntile], CUM[0:64, d, :], ALU.subtract)
                tt(S, "dve", tmp[64:128, 0:ntile], ps[64:128, 256:256 + ntile], CUM[64:128, d, :], ALU.subtract)
                act(S, KTS[:, d, :], tmp[:, 0:ntile], AF.Exp)
                act(S, KBS[:, d, :], CUM[:, d, :], AF.Exp)
                tt(S, "dve", KBS[:, d, :], KBS[:, d, :], BETA[:, d, :], ALU.mult)

            for d in range(2):
                POS = cm[CM_POSF if d == 0 else CM_POSB]
                NEG = cm[CM_NEGF if d == 0 else CM_NEGB]
                memset(S, "dve", Sf[:], 0.0)
                sb_cur = Sb.get()
                memset(S, "pool", sb_cur[:], 0.0)
                nct = CTX // 128
                if d == 0:
                    order = list(range(ntile))
                else:
                    order = list(range(nct - 1, -1, -1)) + list(range(ntile - 1, nct - 1, -1))
                for ti in order:
                    c0 = ti * 128
                    cum = CUM[:, d, ti:ti + 1]
                    dg = f128.get()
                    ts(S, "pool", dg[:], cm[CM_ID][:], cum, ALU.mult)
                    psR = psg.get()
                    mm(S, psR[:], cm[CM_ONES][:], dg[:])
                    a1 = f128.get()
                    stt(S, "dve", a1[:], psR[:], cum, POS[:], ALU.subtract, ALU.add)
                    act(S, a1[:], a1[:], AF.Exp, scale=-1.0)
                    a2 = f128.get()
                    stt(S, "dve", a2[:], psR[:], cum, NEG[:], ALU.subtract, ALU.add)
                    act(S, a2[:], a2[:], AF.Exp)
                    eb = f128.get()
                    act(S, eb[:], psR[:], AF.Exp)
                    qd = b128.get()
                    tt(S, "pool", qd[:], QT[:, c0:c0 + 128], eb[:], ALU.mult)
                    pkk = psg.get()
                    mm(S, pkk[:], KT[:, c0:c0 + 128], KT[:, c0:c0 + 128])
                    pqk = psg.get()
                    mm(S, pqk[:], KT[:, c0:c0 + 128], QT[:, c0:c0 + 128])
                    A = b128.get()
                    stt(S, "dve", A[:], pkk[:], BETA[:, d, ti:ti + 1], a1[:], ALU.mult, ALU.mult)
                    attnT = b128.get()
                    tt(S, "dve", attnT[:], pqk[:], a2[:], ALU.mult)
                    ptB = pst.get()
                    transpose(S, ptB[:], A[:], id_bf[:])
                    Bm = b128.get()
                    copy(S, "act", Bm[:], ptB[:])
                    ptK = pst.get()
                    transpose(S, ptK[:], KT[:, c0:c0 + 128], id_bf[:])
                    kbg = b128.get()
                    act(S, kbg[:], ptK[:], AF.Copy, scale=KBS[:, d, ti:ti + 1])
                    ktm = b128.get()
                    act(S, ktm[:], ptK[:], AF.Copy, scale=KTS[:, d, ti:ti + 1])
                    ptV = pst.get()
                    transpose(S, ptV[:], VT[:, c0:c0 + 128], id_bf[:])
                    bv = b128.get()
                    ts(S, "dve", bv[:], ptV[:], BETA[:, d, ti:ti + 1], ALU.mult)
                    mB = MU_bf if d == 0 else ML_bf
                    mA = ML_bf if d == 0 else MU_bf
                    Al, Bl = [], []
                    for l in range(6):
                        t_a, t_b = b128.get(), b128.get()
                        tt(S, "pool", t_a[:], A[:], mA[l][:], ALU.mult)
                        tt(S, "pool", t_b[:], Bm[:], mB[l][:], ALU.mult)
                        Al.append(t_a)
                        Bl.append(t_b)
                    X = b128.get()
                    XT = b128.get()
                    tt(S, "dve", X[:], id_bf[:], Bl[0][:], ALU.subtract)
                    tt(S, "pool", XT[:], id_bf[:], Al[0][:], ALU.subtract)
                    for l in range(1, 6):
                        last = l == 5
                        pC = psg.get()
                        mm(S, pC[:], Al[l][:], X[:])
                        E = b128.get()
                        tt(S, "dve", E[:], id_bf[:], pC[:], ALU.subtract)
                        pX = psg.get()
                        mm(S, pX[:], XT[:], E[:])
                        Xn = b128.get()
                        copy(S, "act", Xn[:], pX[:])
                        if not last:
                            pC2 = psg.get()
                            mm(S, pC2[:], Bl[l][:], XT[:])
                            E2 = b128.get()
                            tt(S, "dve", E2[:], id_bf[:], pC2[:], ALU.subtract)
                            pXT = psg.get()
                            mm(S, pXT[:], X[:], E2[:])
                            XTn = b128.get()
                            copy(S, "act", XTn[:], pXT[:])
                            XT = XTn
                        X = Xn
                    pW = psg.get()
                    mm(S, pW[:], kbg[:], X[:])
                    nwT = b128.get()
                    act(S, nwT[:], pW[:], AF.Copy, scale=-1.0)
                    pO = psg.get()
                    for j in ((0, 1) if d == 0 else (1, 0)):
                        r0 = 64 * j
                        pV = psg.get()
                        mm(S, pV[:], X[:], bv[:], start=True, stop=False)
                        mm(S, pV[:], nwT[:], sb_cur[:], start=False, stop=True)
                        copy(S, "act", vn[r0:r0 + 64, :], pV[r0:r0 + 64, :])
                        mm(S, pO[:, r0:r0 + 64], sb_cur[:], qd[:, r0:r0 + 64], start=True, stop=False)
                        mm(S, pO[:, r0:r0 + 64], vn[:], attnT[:, r0:r0 + 64], start=False, stop=True)
                        pS = psg.get()
                        mm(S, pS[:], ktm[r0:r0 + 64, :], vn[r0:r0 + 64, :])
                        stt(S, "dve", Sf[:], Sf[:], CD[:, d, j, ti:ti + 1], pS[:], ALU.mult, ALU.add)
                        sb_cur = Sb.get()
                        copy(S, "pool", sb_cur[:], Sf[:])
                    if d == 0:
                        copy(S, "act", OACC[:, c0:c0 + 128], pO[:])
                    else:
                        tt(S, "dve", OACC[:, c0:c0 + 128], pO[:], OACC[:, c0:c0 + 128], ALU.add)
            for ti, (t0, w, _s0, _s1) in enumerate(C.SEQ_TILES):
                zg = ld.get()
                S.dma("act", zg[:, :w], zl[3, hh * 128:(hh + 1) * 128, t0:t0 + w])
                sq = sqb.get()
                tt(S, "pool", sq[:, :w], OACC[:, t0:t0 + w], OACC[:, t0:t0 + w], ALU.mult)
                ps = ps512.get()
                mm(S, ps[:, :w], ones_bf[:], sq[:, :w])
                r = cacc.get()
                act(S, r[:, :w], ps[:, :w], AF.Sqrt, bias=eps_t[:, 0:1], scale=1.0 / 128)
                recip(S, r[:, :w], r[:, :w])
                act(S, zg[:, :w], zg[:, :w], AF.Silu)
                o = cacc.get()
                stt(S, "dve", o[:, :w], OACC[:, t0:t0 + w], hct[:, 15:16], r[:, :w], ALU.mult, ALU.mult)
                ob_t = ost.get()
                tt(S, "pool", ob_t[:, :w], o[:, :w], zg[:, :w], ALU.mult)
                S.dma("sp", io["ob"][ti].ap()[hh * 128:(hh + 1) * 128, 0:w], ob_t[:, :w])
        for ti in range(len(C.SEQ_TILES)):
            all_gather(S, io["ob"][ti], io["og"][ti])
        S.end_phase()


def build_fused(T_=T, max_phase=99):
    C = Cfg(T_)
    nc = new_nc()
    io = {}
    ext = lambda name, shape, dt=F32: nc.dram_tensor(name, list(shape), dt, kind="ExternalInput")
    io["xT"] = ext("xT", [D, C.NT])
    io["mw"] = ext("mw", [D, 6 * D])
    io["mb"] = ext("mb", [128, 48])
    io["cT"] = ext("cT", [128, 16])
    io["nvec"] = ext("nvec", [128, 72])
    io["wl"] = [ext(f"wl{i}", [D, 1032 if KINDS[i] == 0 else 768]) for i in range(DEPTH)]
    io["oh"] = ext("oh", [128, 4])
    io["w_out"] = [ext(f"w_out{i}", [D, D]) for i in range(DEPTH)]
    io["w1"] = [ext(f"w1_{i}", [D, DFF]) for i in range(DEPTH)]
    io["w2"] = [ext(f"w2_{i}", [DFF, D]) for i in range(DEPTH)]
    io["ghc"] = ext("ghc", [2, 2, 128, 16])
    io["scw"] = ext("scw", [128, 6])
    io["dlam"] = ext("dlam", [128, 256])
    io["dng"] = ext("dng", [128, 1])
    io["cs"] = ext("cs", [2, 128, C.T])
    io["cm"] = ext("cm", [NCM, 128, 128])
    io["yo"] = nc.dram_tensor("yo", [D, C.TL], F32, kind="ExternalOutput")
    itn = lambda name, shape, dt: nc.dram_tensor(name, list(shape), dt)
    io["xbuf"] = itn("xbuf", [D, C.NT], F32)
    io["modb"] = itn("modb", [128, 96], F32)
    io["modg"] = itn("modg", [4, 128, 96], F32)
    io["hb"] = [itn(f"hb{k}", [128, C.NT], BF16) for k in range(8)]
    io["hg"] = [itn(f"hg{k}", [4, 128, C.NT], BF16) for k in range(8)]
    io["zl"] = itn("zl", [4, 256, C.NTOK], BF16)
    io["zabl"] = itn("zabl", [4, 2, C.NTOK], F32)
    io["ob"] = [itn(f"ob{k}", [256, w_], BF16) for k, (t0_, w_, a_, b_) in enumerate(C.SEQ_TILES)]
    io["og"] = [itn(f"og{k}", [4, 256, w_], BF16) for k, (t0_, w_, a_, b_) in enumerate(C.SEQ_TILES)]
    with ExitStack() as st:
        S = Sched(nc, st)
        plist = [lambda: phase_M(S, io, C)]
        for i in range(DEPTH + 1):
            plist.append(lambda i=i: phase_T(S, io, C, i))
            if i == DEPTH:
                break
            plist.append(lambda i=i: phase_inproj(S, io, C, i))
            if KINDS[i] == 0:
                plist.append(lambda i=i: phase_H_gdn(S, io, C, i // 3))
            elif KINDS[i] == 1:
                plist.append(lambda: phase_H_sconv(S, io, C))
            else:
                plist.append(lambda: phase_H_diff(S, io, C))
        for f in plist[:max_phase]:
            f()
    return nc


def chunkvec(v):
    return np.ascontiguousarray(np.asarray(v, np.float32).reshape(8, 128).T)


def _c(a):
    return np.ascontiguousarray(a, dtype=np.float32)


def make_in_maps(inp, T_=T):
    C = Cfg(T_)
    cmc = consts_cm()
    cs = rope_tables(C.T)
    nvec = np.zeros((128, 9, 8), np.float32)
    for i in range(4):
        nvec[:, i] = chunkvec(inp["norm1_g"][i])
        nvec[:, 4 + i] = chunkvec(inp["norm2_g"][i])
    nvec[:, 8] = chunkvec(inp["final_g"])
    nvec = _c(nvec.reshape(128, 72))
    w_ins = [inp["gdn_w_in"][0], inp["sconv_w_in"][0], inp["diff_w_in"][0], inp["gdn_w_in"][1]]
    w_outs = [inp["gdn_w_out"][0], inp["sconv_w_out"][0], inp["diff_w_out"][0], inp["gdn_w_out"][1]]
    shared = {"nvec": nvec, "cs": cs, "cm": cmc,
              "dlam": _c(np.broadcast_to(inp["diff_lambda"][0].reshape(1, 256), (128, 256))),
              "dng": _c(inp["diff_norm_g"][0].reshape(128, 1))}
    for i in range(DEPTH):
        shared[f"w_out{i}"] = _c(w_outs[i])
        shared[f"w1_{i}"] = _c(inp["mlp_w1"][i])
        shared[f"w2_{i}"] = _c(inp["mlp_w2"][i])
    maps = []
    for cc in range(NCORE):
        b, r = cc // 4, cc % 4
        m = dict(shared)
        m["xT"] = _c(np.concatenate([inp["x"][b, r * C.TL:(r + 1) * C.TL].T, inp["ctx"][b, r * C.TC:(r + 1) * C.TC].T], axis=1))
        m["mw"] = _c(inp["ada_w"][r])
        m["mb"] = _c(inp["ada_b"][r].reshape(48, 128).T)
        c2 = np.stack([inp["c"][b], inp["c_ctx"]], axis=0)
        m["cT"] = _c(c2.T.reshape(8, 128, 2).transpose(1, 0, 2).reshape(128, 16))
        ghc = np.zeros((2, 2, 128, 16), np.float32)
        for j in range(2):
            for hh in range(2):
                h = 2 * r + hh
                for xi in range(3):
                    for tap in range(3):
                        ghc[j, hh, :, 3 * xi + tap] = inp["gdn_conv"][j, tap, xi * 1024 + h * 128:xi * 1024 + (h + 1) * 128]
                ghc[j, hh, :, 9:11] = inp["gdn_a_log"][j, :, h][None, :]
                ghc[j, hh, :, 11:13] = inp["gdn_dt_bias"][j, :, h][None, :]
                ghc[j, hh, :, 15] = inp["gdn_norm_g"][j]
        m["ghc"] = ghc
        cw = np.empty((128, 2, 3), np.float32)
        for jj in range(2):
            ch0 = 256 * r + 128 * jj
            cw[:, jj, :] = inp["sconv_conv"][0][:, ch0:ch0 + 128].T
        m["scw"] = _c(cw.reshape(128, 6))
        ohm = np.zeros((128, 4), np.float32)
        ohm[:, r] = 1.0
        m["oh"] = ohm
        for i in range(DEPTH):
            if KINDS[i] == 0:
                cols = [xi * 1024 + (2 * r + hh) * 128 + p_ for xi in range(4) for hh in range(2) for p_ in range(128)]
                cols += [4096 + kd * 8 + 2 * r + hh for kd in range(4) for hh in range(2)]
            else:
                cols = [part * 1024 + r * 256 + q_ for part in range(3) for q_ in range(256)]
            m[f"wl{i}"] = _c(w_ins[i][:, cols])
        maps.append(m)
    return maps


_NC = {}


def kernel(x, c, ctx, c_ctx, norm1_g, norm2_g, ada_w, ada_b, mlp_w1, mlp_w2,
           gdn_w_in, gdn_conv, gdn_a_log, gdn_dt_bias, gdn_norm_g, gdn_w_out,
           sconv_w_in, sconv_conv, sconv_w_out,
           diff_w_in, diff_lambda, diff_norm_g, diff_w_out, final_g):
    inp = dict(x=x, c=c, ctx=ctx, c_ctx=c_ctx, norm1_g=norm1_g, norm2_g=norm2_g, ada_w=ada_w, ada_b=ada_b,
               mlp_w1=mlp_w1, mlp_w2=mlp_w2, gdn_w_in=gdn_w_in, gdn_conv=gdn_conv, gdn_a_log=gdn_a_log,
               gdn_dt_bias=gdn_dt_bias, gdn_norm_g=gdn_norm_g, gdn_w_out=gdn_w_out, sconv_w_in=sconv_w_in,
               sconv_conv=sconv_conv, sconv_w_out=sconv_w_out, diff_w_in=diff_w_in, diff_lambda=diff_lambda,
               diff_norm_g=diff_norm_g, diff_w_out=diff_w_out, final_g=final_g)
    inp = {k: np.asarray(v, dtype=np.float32) for k, v in inp.items()}
    if "nc" not in _NC:
        _NC["nc"] = build_fused(T)
    maps = make_in_maps(inp, T)
    res = run_bass_kernel_spmd(_NC["nc"], maps, core_ids=list(range(NCORE))).results
    out = np.empty((B, T, D), np.float32)
    for cc in range(NCORE):
        b, r = cc // 4, cc % 4
        out[b, r * TL:(r + 1) * TL] = res[cc]["yo"].T
    return out
```
